# Optimizing a Trainium2 kernel written in Bass

```python
import math
import jax, jax.numpy as jnp
from jax import lax
import numpy as np

D_MODEL = 1024
BATCH = 32
SEQ = 256
DEPTH = 1
DEC_BATCH = 4
DEC_SEQ = 1024
PAST_LEN = 256

GRID_W = 64
NA_HEADS = 8
HEAD_DIM = 64
NA_WIDTH = NA_HEADS * HEAD_DIM
WIN_R = 8
WIN_C = 16
ROPE_BASE = 10000.0
DN_HEADS = 8
DN_DK = 64
DN_DV = 64
DN_QK_WIDTH = DN_HEADS * DN_DK
DN_V_WIDTH = DN_HEADS * DN_DV
DN_CONV_CH = 2 * DN_QK_WIDTH + DN_V_WIDTH
CONV_W = 5
CHUNK = 64
D_FF = 4 * D_MODEL
N_MOD = 6
EPS = 1e-6
NEG_INF = -1e30
IN_SIZES = (NA_WIDTH, NA_WIDTH, NA_WIDTH, DN_CONV_CH, 2 * DN_HEADS, 2 * DN_HEADS, DN_V_WIDTH, D_MODEL, D_MODEL)
IN_COLS = 3 * NA_WIDTH + DN_CONV_CH + 4 * DN_HEADS + DN_V_WIDTH + 2 * D_MODEL

kernel_name = 'neighbourhood_deltanet_flow_step'


def _rms(x, w):
    xf = x.astype(jnp.float32)
    y = xf * lax.rsqrt(jnp.mean(xf * xf, axis=-1, keepdims=True) + EPS)
    return (y * w.astype(jnp.float32)).astype(x.dtype)


def _l2n(x):
    xf = x.astype(jnp.float32)
    return xf * lax.rsqrt(jnp.sum(xf * xf, axis=-1, keepdims=True) + EPS)


def _adaln(cvec, w_ada, b_ada):
    m = jax.nn.silu(cvec) @ w_ada + b_ada
    m = m.reshape(m.shape[:-1] + (1, N_MOD, D_MODEL))
    return [m[..., i, :] for i in range(N_MOD)]


def _axial_rope(x):
    T = x.shape[1]
    pos = jnp.arange(T)
    half = HEAD_DIM // 2
    nf = half // 2
    inv = ROPE_BASE ** (-jnp.arange(nf, dtype=jnp.float32) / nf)

    def rot(xa, p):
        ang = p.astype(jnp.float32)[:, None] * inv
        cos = jnp.cos(ang)[None, :, None, :]
        sin = jnp.sin(ang)[None, :, None, :]
        x1, x2 = xa[..., :nf], xa[..., nf:]
        return jnp.concatenate([x1 * cos - x2 * sin, x1 * sin + x2 * cos], axis=-1)

    xf = x.astype(jnp.float32)
    out = jnp.concatenate([rot(xf[..., :half], pos // GRID_W), rot(xf[..., half:], pos % GRID_W)], axis=-1)
    return out.astype(x.dtype)


def _dwconv(x, w):
    return lax.conv_general_dilated(
        x, w[:, None, :].astype(x.dtype), window_strides=(1,),
        padding=[(CONV_W // 2, CONV_W // 2)],
        dimension_numbers=('NWC', 'WIO', 'NWC'),
        feature_group_count=x.shape[-1])


def _project(h, w_in, conv_w):
    B, T, _ = h.shape
    p = h @ w_in
    idx = [int(i) for i in np.cumsum(IN_SIZES)[:-1]]
    na_q, na_k, na_v, dn_qkv, dn_b, dn_a, dn_z, gate_na, gate_dn = jnp.split(p, idx, axis=-1)
    dn_qkv = jax.nn.silu(_dwconv(dn_qkv, conv_w))
    dq, dk, dv = jnp.split(dn_qkv, [DN_QK_WIDTH, 2 * DN_QK_WIDTH], axis=-1)
    na_q = na_q.reshape(B, T, NA_HEADS, HEAD_DIM)
    na_k = na_k.reshape(B, T, NA_HEADS, HEAD_DIM)
    na_v = na_v.reshape(B, T, NA_HEADS, HEAD_DIM)
    dq = _l2n(dq.reshape(B, T, DN_HEADS, DN_DK))
    dk = _l2n(dk.reshape(B, T, DN_HEADS, DN_DK))
    dv = dv.reshape(B, T, DN_HEADS, DN_DV)
    return na_q, na_k, na_v, dq, dk, dv, dn_b, dn_a, dn_z, gate_na, gate_dn


def _na_context(q, k, v):
    s = jnp.einsum('bqhd,bkhd->bhqk', q, k).astype(jnp.float32) * (HEAD_DIM ** -0.5)
    p = jax.nn.softmax(s, axis=-1).astype(v.dtype)
    return jnp.einsum('bhqk,bkhd->bqhd', p, v)


def _na_latent(q, k, v, ck, cv, rpb):
    B, T, H, Dh = q.shape
    rows = T // GRID_W
    kr = min(WIN_R, rows)
    r = np.arange(rows)
    rs = np.clip(r - kr // 2, 0, rows - kr)
    krow = rs[:, None] + np.arange(kr)
    col = np.arange(GRID_W)
    cs = np.clip(col - WIN_C // 2, 0, GRID_W - WIN_C)
    kcol = np.broadcast_to(col, (kr, GRID_W)).reshape(-1)
    valid = (kcol[None, :] >= cs[:, None]) & (kcol[None, :] < cs[:, None] + WIN_C)
    dr_idx = (np.repeat(krow, GRID_W, axis=1) - r[:, None]) + (WIN_R - 1)
    dc_idx = np.clip(kcol[None, :] - col[:, None] + (WIN_C - 1), 0, 2 * WIN_C - 2)
    bias = rpb[:, dr_idx[:, None, :], dc_idx[None, :, :]].astype(jnp.float32)
    K = kr * GRID_W
    qg = q.reshape(B, rows, GRID_W, H, Dh)
    kg = k.reshape(B, rows, GRID_W, H, Dh)[:, krow].reshape(B, rows, K, H, Dh)
    vg = v.reshape(B, rows, GRID_W, H, Dh)[:, krow].reshape(B, rows, K, H, Dh)
    scale = HEAD_DIM ** -0.5
    s_loc = jnp.einsum('brqhd,brkhd->bhrqk', qg, kg).astype(jnp.float32) * scale + bias[None]
    s_loc = jnp.where(valid[None, None, None], s_loc, NEG_INF)
    s_ctx = jnp.einsum('brqhd,bhkd->bhrqk', qg, ck).astype(jnp.float32) * scale
    p = jax.nn.softmax(jnp.concatenate([s_loc, s_ctx], axis=-1), axis=-1).astype(v.dtype)
    o = jnp.einsum('bhrqk,brkhd->brqhd', p[..., :K], vg) + jnp.einsum('bhrqk,bhkd->brqhd', p[..., K:], cv)
    return o.reshape(B, T, H, Dh)


def _gated_delta_chunked(q, k, v, g, beta, s0):
    B, T, H, DK = q.shape
    DV = v.shape[-1]
    n = T // CHUNK
    f32 = jnp.float32

    def blk(a):
        a = a.astype(f32).reshape((B, n, CHUNK, H) + a.shape[3:])
        return jnp.moveaxis(a, (1, 3), (0, 2))

    qb = blk(q) * (DK ** -0.5)
    kb, vb, gb, bb = blk(k), blk(v), blk(g), blk(beta)
    gc = jnp.cumsum(gb, axis=-1)
    tril = jnp.tril(jnp.ones((CHUNK, CHUNK), bool))
    strict = jnp.tril(jnp.ones((CHUNK, CHUNK), bool), -1)
    diff = gc[..., :, None] - gc[..., None, :]
    decay = jnp.where(tril, jnp.exp(jnp.where(tril, diff, 0.0)), 0.0)
    kbeta = kb * bb[..., None]
    a_mat = jnp.where(strict, jnp.einsum('nbhid,nbhjd->nbhij', kbeta, kb) * decay, 0.0) + jnp.eye(CHUNK, dtype=f32)
    rhs = jnp.concatenate([vb * bb[..., None], kbeta * jnp.exp(gc)[..., None]], axis=-1)
    sol = lax.linalg.triangular_solve(a_mat, rhs, left_side=True, lower=True, unit_diagonal=True)
    u, w = sol[..., :DV], sol[..., DV:]
    qk = jnp.einsum('nbhid,nbhjd->nbhij', qb, kb) * decay

    def step(S, xs):
        qc, kc, uc, wc, gcc, qkc = xs
        v_new = uc - jnp.einsum('bhck,bhkv->bhcv', wc, S)
        o = (jnp.einsum('bhck,bhkv->bhcv', qc * jnp.exp(gcc)[..., None], S)
             + jnp.einsum('bhij,bhjv->bhiv', qkc, v_new))
        glast = gcc[..., -1:]
        S = S * jnp.exp(glast)[..., None] + jnp.einsum('bhck,bhcv->bhkv', kc * jnp.exp(glast - gcc)[..., None], v_new)
        return S, o

    s_fin, o = lax.scan(step, s0.astype(f32), (qb, kb, u, w, gc, qk))
    o = jnp.moveaxis(o, (0, 2), (1, 3)).reshape(B, T, H, DV)
    return o.astype(v.dtype), s_fin


def _delta_dir(q, k, v, b_logit, a_logit, a_log, dt_bias, s0, reverse):
    beta = jax.nn.sigmoid(b_logit.astype(jnp.float32))
    g = -jnp.exp(a_log.astype(jnp.float32)) * jax.nn.softplus(a_logit.astype(jnp.float32) + dt_bias.astype(jnp.float32))
    if reverse:
        q, k, v, beta, g = (jnp.flip(t, axis=1) for t in (q, k, v, beta, g))
    o, s = _gated_delta_chunked(q, k, v, g, beta, s0)
    if reverse:
        o = jnp.flip(o, axis=1)
    return o, s


def _merge(o_na, o_dn, z, gate_na, gate_dn, dn_norm, w_ao, w_do, w_out):
    B, T = o_na.shape[:2]
    o_dn = _rms(o_dn, dn_norm) * jax.nn.silu(z).reshape(B, T, DN_HEADS, DN_DV)
    br_na = o_na.reshape(B, T, NA_WIDTH) @ w_ao
    br_dn = o_dn.reshape(B, T, DN_V_WIDTH) @ w_do
    m = jax.nn.sigmoid(gate_na) * br_na + jax.nn.sigmoid(gate_dn) * br_dn
    return m @ w_out


def _mixer_context(h, w_in, conv_w, a_log, dt_bias, dn_norm, w_ao, w_do, w_out):
    q, k, v, dq, dk, dv, b, a, z, ga, gd = _project(h, w_in, conv_w)
    o_na = _na_context(q, k, v)
    s0 = jnp.zeros((h.shape[0], DN_HEADS, DN_DK, DN_DV), jnp.float32)
    o_f, s_f = _delta_dir(dq, dk, dv, b[..., :DN_HEADS], a[..., :DN_HEADS], a_log[0], dt_bias[0], s0, False)
    o_b, s_b = _delta_dir(dq, dk, dv, b[..., DN_HEADS:], a[..., DN_HEADS:], a_log[1], dt_bias[1], s0, True)
    y = _merge(o_na, o_f + o_b, z, ga, gd, dn_norm, w_ao, w_do, w_out)
    return y, jnp.transpose(k, (0, 2, 1, 3)), jnp.transpose(v, (0, 2, 1, 3)), jnp.stack([s_f, s_b], axis=1)


def _mixer_latent(h, ck, cv, st, w_in, conv_w, a_log, dt_bias, dn_norm, rpb, w_ao, w_do, w_out):
    q, k, v, dq, dk, dv, b, a, z, ga, gd = _project(h, w_in, conv_w)
    o_na = _na_latent(_axial_rope(q), _axial_rope(k), v, ck, cv, rpb)
    s = st.astype(jnp.float32)
    o_f, _ = _delta_dir(dq, dk, dv, b[..., :DN_HEADS], a[..., :DN_HEADS], a_log[0], dt_bias[0], s[:, 0], False)
    o_b, _ = _delta_dir(dq, dk, dv, b[..., DN_HEADS:], a[..., DN_HEADS:], a_log[1], dt_bias[1], s[:, 1], True)
    return _merge(o_na, o_f + o_b, z, ga, gd, dn_norm, w_ao, w_do, w_out)


def _ffn_sub(x, shift, scale, gate, w_pre, w_post, w1, w2):
    h = _rms(x, w_pre) * (1 + scale) + shift
    f = jnp.square(jax.nn.relu(h @ w1)) @ w2
    return x + gate * _rms(f, w_post)


def setup_inputs(seed: int = 0) -> dict:
    key = jax.random.key(seed)
    ks = jax.random.split(key, 24)
    f32 = jnp.float32

    def nrm(k, shape, s):
        return jax.random.normal(k, shape, f32) * s

    dt = jnp.exp(jax.random.uniform(ks[16], (DEPTH, 2, DN_HEADS), f32, math.log(1e-3), math.log(1e-1)))
    return {
        'x_prompt': nrm(ks[0], (BATCH, SEQ, D_MODEL), 1.0),
        'x_sample': nrm(ks[1], (DEC_BATCH, DEC_SEQ, D_MODEL), 1.0),
        'c': nrm(ks[2], (DEC_BATCH, D_MODEL), 1.0),
        'cache_na_k': nrm(ks[3], (DEC_BATCH, DEPTH, NA_HEADS, PAST_LEN, HEAD_DIM), 1.0),
        'cache_na_v': nrm(ks[4], (DEC_BATCH, DEPTH, NA_HEADS, PAST_LEN, HEAD_DIM), 1.0),
        'state_delta': nrm(ks[5], (DEC_BATCH, DEPTH, 2, DN_HEADS, DN_DK, DN_DV), DN_DK ** -0.5),
        'c_ctx': nrm(ks[6], (D_MODEL,), 1.0),
        'w_ada': nrm(ks[7], (DEPTH, D_MODEL, N_MOD * D_MODEL), 0.5 * D_MODEL ** -0.5),
        'b_ada': nrm(ks[8], (DEPTH, N_MOD * D_MODEL), 0.02),
        'norm_pre1': 1.0 + nrm(ks[9], (DEPTH, D_MODEL), 0.05),
        'norm_post1': 1.0 + nrm(ks[10], (DEPTH, D_MODEL), 0.05),
        'norm_pre2': 1.0 + nrm(ks[11], (DEPTH, D_MODEL), 0.05),
        'norm_post2': 1.0 + nrm(ks[12], (DEPTH, D_MODEL), 0.05),
        'w_in': nrm(ks[13], (DEPTH, D_MODEL, IN_COLS), D_MODEL ** -0.5),
        'conv_w': nrm(ks[14], (DEPTH, CONV_W, DN_CONV_CH), CONV_W ** -0.5),
        'a_log': jnp.log(jax.random.uniform(ks[15], (DEPTH, 2, DN_HEADS), f32, 1.0, 16.0)),
        'dt_bias': dt + jnp.log(-jnp.expm1(-dt)),
        'dn_norm': 1.0 + nrm(ks[17], (DEPTH, DN_DV), 0.05),
        'na_rpb': nrm(ks[18], (DEPTH, NA_HEADS, 2 * WIN_R - 1, 2 * WIN_C - 1), 0.02),
        'w_ao': nrm(ks[19], (DEPTH, NA_WIDTH, D_MODEL), NA_WIDTH ** -0.5),
        'w_do': nrm(ks[20], (DEPTH, DN_V_WIDTH, D_MODEL), DN_V_WIDTH ** -0.5),
        'w_out': nrm(ks[21], (DEPTH, D_MODEL, D_MODEL), D_MODEL ** -0.5),
        'w_ff1': nrm(ks[22], (DEPTH, D_MODEL, D_FF), D_MODEL ** -0.5),
        'w_ff2': nrm(ks[23], (DEPTH, D_FF, D_MODEL), D_FF ** -0.5),
    }


def reference(x_prompt, x_sample, c, cache_na_k, cache_na_v, state_delta, c_ctx,
              w_ada, b_ada, norm_pre1, norm_post1, norm_pre2, norm_post2,
              w_in, conv_w, a_log, dt_bias, dn_norm, na_rpb,
              w_ao, w_do, w_out, w_ff1, w_ff2):
    xp = x_prompt
    xs = x_sample
    new_k, new_v, new_s = [], [], []
    for l in range(DEPTH):
        sh1, sc1, g1, sh2, sc2, g2 = _adaln(c_ctx, w_ada[l], b_ada[l])
        h = _rms(xp, norm_pre1[l]) * (1 + sc1) + sh1
        y, kc, vc, st = _mixer_context(h, w_in[l], conv_w[l], a_log[l], dt_bias[l], dn_norm[l],
                                       w_ao[l], w_do[l], w_out[l])
        xp = xp + g1 * _rms(y, norm_post1[l])
        xp = _ffn_sub(xp, sh2, sc2, g2, norm_pre2[l], norm_post2[l], w_ff1[l], w_ff2[l])
        new_k.append(kc)
        new_v.append(vc)
        new_s.append(st.astype(x_prompt.dtype))
        sh1, sc1, g1, sh2, sc2, g2 = _adaln(c, w_ada[l], b_ada[l])
        h = _rms(xs, norm_pre1[l]) * (1 + sc1) + sh1
        y = _mixer_latent(h, cache_na_k[:, l], cache_na_v[:, l], state_delta[:, l], w_in[l], conv_w[l],
                          a_log[l], dt_bias[l], dn_norm[l], na_rpb[l], w_ao[l], w_do[l], w_out[l])
        xs = xs + g1 * _rms(y, norm_post1[l])
        xs = _ffn_sub(xs, sh2, sc2, g2, norm_pre2[l], norm_post2[l], w_ff1[l], w_ff2[l])
    new_cache_na_k = jnp.stack(new_k, axis=1)
    new_cache_na_v = jnp.stack(new_v, axis=1)
    new_state_delta = jnp.stack(new_s, axis=1)
    return (xp, xs, new_cache_na_k, new_cache_na_v, new_state_delta)
```

```python
import contextlib
import os
import numpy as np
import concourse.bass as bass
import concourse.mybir as mybir
from concourse.bass_utils import run_bass_kernel_spmd

F32 = mybir.dt.float32
BF16 = mybir.dt.bfloat16
AF = mybir.ActivationFunctionType
ALU = mybir.AluOpType
AX = mybir.AxisListType

DEBUG = False
STAGE = 99


class StopBuild(Exception):
    pass
NEG = -1.0e30
EPS = 1e-6
IN_OFF = dict(q=0, k=512, v=1024, dn=1536, b=3072, a=3088, z=3104, ga=3616, gd=4640)


class Chan:
    def __init__(self, sem, step):
        self.sem, self.step, self.count = sem, step, 0


class Dep:
    __slots__ = ("w", "r")

    def __init__(self):
        self.w = None
        self.r = {}


class Buf:
    def __init__(self, t, keys=None):
        self.t = t
        self.d = {k: Dep() for k in (keys if keys is not None else [None])}

    def __getitem__(self, idx):
        return self.t[idx]

    def D(self, *keys):
        if not keys:
            return list(self.d.values())
        return [self.d[k] for k in keys]


class Eng:
    def __init__(self, name, obj, chan):
        self.name, self.obj, self.chan, self.known = name, obj, chan, {}

    def wait_for(self, chan, value):
        if value <= 0 or self.known.get(chan, 0) >= value:
            return
        self.obj.wait_ge(chan.sem, value)
        self.known[chan] = value


def _deps(lst):
    out = []
    for x in lst:
        if isinstance(x, Buf):
            out.extend(x.D())
        elif isinstance(x, (list, tuple)):
            out.extend(_deps(x))
        else:
            out.append(x)
    return out


class KB:
    def __init__(self):
        self.nc = nc = bass.Bass("TRN2", target_bir_lowering=False)
        self.es = contextlib.ExitStack()
        self.dbg = []

    def start(self):
        nc, es = self.nc, self.es
        sem = lambda n: es.enter_context(nc.semaphore(n))
        self.E = {}
        for nm, ob in [("pe", nc.tensor), ("act", nc.scalar), ("dve", nc.vector), ("pool", nc.gpsimd), ("sp", nc.sync)]:
            self.E[nm] = Eng(nm, ob, Chan(sem("s_" + nm), 1))
        self.dch = [Chan(sem("d%d" % i), 16) for i in range(int(os.environ.get("KNCH", "16")))]
        self.dch_i = 0
        self.dch_ip = 0
        self.outch = Chan(sem("outc"), 16)
        self.PA = Buf(es.enter_context(nc.psum_tensor("PA", [128, 2048], F32)), keys=[0, 1, 2, 3])
        self.PB = Buf(es.enter_context(nc.psum_tensor("PB", [128, 1024], F32)), keys=[0, 1])
        self.PC = Buf(es.enter_context(nc.psum_tensor("PC", [128, 512], F32)), keys=[0])
        self.PD = Buf(es.enter_context(nc.psum_tensor("PD", [128, 512], F32)), keys=[0])
        self.banks = [(self.PA, 0), (self.PA, 1), (self.PA, 2), (self.PA, 3), (self.PB, 0), (self.PB, 1), (self.PC, 0), (self.PD, 0)]
        self.bank_i = 0
        es.enter_context(nc.Block())

    def bank(self, i=None):
        if i is None:
            i = self.bank_i
            self.bank_i = (self.bank_i + 1) % 8
        b, k = self.banks[i]
        return b[:, k * 512:(k + 1) * 512], b.d[k]

    def sb(self, es, name, shape, dt, keys=None):
        self.uid = getattr(self, "uid", 0) + 1
        return Buf(es.enter_context(self.nc.sbuf_tensor("%s_%d" % (name, self.uid), shape, dt)), keys)

    stopped = False

    def emit(self, en, fn, R=(), W=(), chan=None):
        if self.stopped:
            return None
        eng = self.E[en]
        c = chan if chan is not None else eng.chan
        R, W = _deps(R), _deps(W)
        need = {}
        for d in R:
            if d.w is not None:
                need[d.w[0]] = max(need.get(d.w[0], 0), d.w[1])
        for d in W:
            if d.w is not None:
                need[d.w[0]] = max(need.get(d.w[0], 0), d.w[1])
            for ch, v in d.r.items():
                need[ch] = max(need.get(ch, 0), v)
        for ch, v in need.items():
            if ch is eng.chan and en == "pe":
                continue
            eng.wait_for(ch, v)
        ins = fn()
        c.count += c.step
        ins.then_inc(c.sem, c.step)
        for d in R:
            d.r[c] = c.count
        for d in W:
            d.w = (c, c.count)
            d.r = {}
        return ins

    def barrier(self):
        if self.stopped:
            return
        chans = [e.chan for e in self.E.values()] + self.dch + [self.outch]
        for e in self.E.values():
            for ch in chans:
                if ch is e.chan:
                    continue
                e.wait_for(ch, ch.count)

    def mm(self, out, lhsT, rhs, start, stop, R, W):
        nc = self.nc
        return self.emit("pe", lambda: nc.tensor.matmul(out, lhsT=lhsT, rhs=rhs, start=start, stop=stop), R, W)

    def tr(self, out, in_, ident, R, W):
        nc = self.nc
        return self.emit("pe", lambda: nc.tensor.transpose(out, in_, ident), R, W)

    def act(self, out, in_, func, R, W, scale=None, bias=None, accum=None):
        nc = self.nc
        kw = {}
        if scale is not None:
            kw["scale"] = scale
        if bias is not None:
            kw["bias"] = bias
        if accum is not None:
            kw["accum_out"] = accum
        return self.emit("act", lambda: nc.scalar.activation(out, in_, func, **kw), R, W)

    def _v(self, en):
        return self.nc.vector if en == "dve" else self.nc.gpsimd

    def tt(self, en, out, a, b, op, R, W):
        e = self._v(en)
        return self.emit(en, lambda: e.tensor_tensor(out, a, b, op), R, W)

    def ts(self, en, out, a, s1, s2, op0, op1, R, W):
        e = self._v(en)
        if op1 is None:
            return self.emit(en, lambda: e.tensor_scalar(out, a, s1, None, op0), R, W)
        return self.emit(en, lambda: e.tensor_scalar(out, a, s1, s2, op0, op1), R, W)

    def stt(self, out, in0, scalar, in1, op0, op1, R, W):
        nc = self.nc
        return self.emit("dve", lambda: nc.vector.scalar_tensor_tensor(out, in0, scalar, in1, op0, op1), R, W)

    def cp(self, en, out, in_, R, W):
        nc = self.nc
        if en == "act":
            if os.environ.get("KACTCP", "1") == "1":
                return self.emit("act", lambda: nc.scalar.activation(out, in_, AF.Identity), R, W)
            return self.emit("act", lambda: nc.scalar.copy(out, in_), R, W)
        e = self._v(en)
        return self.emit(en, lambda: e.tensor_copy(out, in_), R, W)

    def memset(self, en, ap, val, W):
        e = self._v(en)
        return self.emit(en, lambda: e.memset(ap, val), (), W)

    def recip(self, out, in_, R, W):
        nc = self.nc
        return self.emit("dve", lambda: nc.vector.reciprocal(out, in_), R, W)

    def reduce_sum(self, out, in_, R, W):
        nc = self.nc
        return self.emit("dve", lambda: nc.vector.tensor_reduce(out, in_, AX.X, ALU.add), R, W)

    def dma(self, en, out, in_, R, W, chan=None, slow=False):
        eng = self.E[en]
        if chan is None:
            half = len(self.dch) // 2
            if en == "pool":
                chan = self.dch[half + self.dch_ip]
                self.dch_ip = (self.dch_ip + 1) % (len(self.dch) - half)
            else:
                chan = self.dch[self.dch_i]
                self.dch_i = (self.dch_i + 1) % half
            if not self.stopped:
                eng.wait_for(chan, chan.count)
        o = eng.obj
        if slow:
            r = self.emit(en, lambda: o.dma_start(out=out, in_=in_, allow_slow_non_contiguous=True), R, W, chan=chan)
        else:
            r = self.emit(en, lambda: o.dma_start(out=out, in_=in_), R, W, chan=chan)
        if os.environ.get("KSYNC", "0") == "1" and not self.stopped:
            eng.wait_for(chan, chan.count)
        return r

    def dump(self, name, ap, R):
        if self.stopped:
            return
        if not DEBUG and name not in os.environ.get("KDUMP", "").split(","):
            return
        shape = list(ap.shape)
        dt = ap.dtype
        o = self.nc.dram_tensor("dbg_" + name, shape, dt, kind="ExternalOutput").ap()
        self.dma("sp", o, ap, R, [])
        self.dbg.append("dbg_" + name)


def _rope_tables():
    t = np.arange(1024)
    row, col = t // 64, t % 64
    inv = 10000.0 ** (-np.arange(16, dtype=np.float32) / 16.0)
    cos = np.zeros((128, 1024), np.float32)
    sins = np.zeros((128, 1024), np.float32)
    for p in range(128):
        d = p % 64
        pos = row if d < 32 else col
        ang = pos.astype(np.float32) * inv[d % 16]
        cos[p] = np.cos(ang)
        s = np.sin(ang)
        sins[p] = -s if (d % 32) < 16 else s
    return cos, sins


def _rope_perm():
    perm = np.zeros(1024, np.int64)
    for c in range(1024):
        blk, hh, d = c // 512, (c % 512) // 64, c % 64
        pd = d + 16 if (d % 32) < 16 else d - 16
        perm[c] = blk * 512 + hh * 64 + pd
    return perm


def _rs(r):
    return int(np.clip(r - 4, 0, 8))


def _cs(c):
    return int(np.clip(c - 8, 0, 48))


def na_tiles(a):
    lo = _rs(2 * a) // 2
    hi = (_rs(2 * a + 1) + 7) // 2
    return list(range(lo, hi + 1))


def _na_masks():
    tiles = []
    index = {}
    for a in range(8):
        for kt in na_tiles(a):
            m = np.full((128, 128), NEG, np.float32)
            for p in range(128):
                kr, kc = 2 * kt + p // 64, p % 64
                for q in range(128):
                    qr, qc = 2 * a + q // 64, q % 64
                    if _rs(qr) <= kr < _rs(qr) + 8 and _cs(qc) <= kc < _cs(qc) + 16:
                        m[p, q] = 0.0
            index[(a, kt)] = len(tiles)
            tiles.append(m)
    return np.stack(tiles, axis=1), index


_NA_MASKS, NA_MASK_INDEX = None, None


def _get_na_masks():
    global _NA_MASKS, NA_MASK_INDEX
    if _NA_MASKS is None:
        tiles, index = [], {}
        p = np.arange(128)
        q = np.arange(128)
        for a in range(8):
            for kt in na_tiles(a):
                kr = (2 * kt + p // 64)[:, None]
                kc = (p % 64)[:, None]
                qr = (2 * a + q // 64)[None, :]
                qc = (q % 64)[None, :]
                rs = np.clip(qr - 4, 0, 8)
                cs = np.clip(qc - 8, 0, 48)
                valid = (kr >= rs) & (kr < rs + 8) & (kc >= cs) & (kc < cs + 16)
                index[(a, kt)] = len(tiles)
                tiles.append(np.where(valid, 0.0, NEG).astype(np.float32))
        _NA_MASKS, NA_MASK_INDEX = np.ascontiguousarray(np.stack(tiles, axis=1)), index
    return _NA_MASKS, NA_MASK_INDEX


def _rpb_table(rpb):
    out = np.zeros((128, 8, 16, 64), np.float32)
    kc = np.arange(64)[:, None]
    qc = np.arange(64)[None, :]
    dc = np.clip(kc - qc + 15, 0, 30)
    for m in range(16):
        if 0 <= 14 - m <= 14:
            out[0:64, :, m, :] = np.transpose(rpb[:, 14 - m][:, dc], (1, 0, 2))
        if 0 <= 15 - m <= 14:
            out[64:128, :, m, :] = np.transpose(rpb[:, 15 - m][:, dc], (1, 0, 2))
    return out


def _dn_masks():
    tri = np.zeros((2, 128, 128), np.float32)
    negi = np.full((2, 128, 128), NEG, np.float32)
    negs = np.full((2, 128, 128), NEG, np.float32)
    j = np.arange(64)[:, None]
    i = np.arange(64)[None, :]
    for v in range(2):
        for half in range(2):
            fwd = (half == v)
            sl = slice(half * 64, half * 64 + 64)
            if fwd:
                tri[v, sl, sl] = (j <= i)
                negi[v, sl, sl] = np.where(i >= j, 0.0, NEG)
                negs[v, sl, sl] = np.where(i > j, 0.0, NEG)
            else:
                tri[v, sl, sl] = (j >= i)
                negi[v, sl, sl] = np.where(i <= j, 0.0, NEG)
                negs[v, sl, sl] = np.where(i < j, 0.0, NEG)
    return tri, negi, negs


def build_program():
    kb = KB()
    nc = kb.nc

    def din(name, shape, dt=F32):
        return nc.dram_tensor(name, list(shape), dt, kind="ExternalInput").ap()

    def dout(name, shape, dt=F32):
        return nc.dram_tensor(name, list(shape), dt, kind="ExternalOutput").ap()

    X = [din("xc", [1024, 1024]), din("xl", [1024, 1024])]
    cvec = din("cvec", [2, 1024])
    ckd = din("ck", [8, 256, 64])
    cvd = din("cv", [8, 256, 64])
    std = din("st", [2, 8, 64, 64])
    w_ada = din("w_ada", [1024, 6144])
    b_ada = din("b_ada", [6144])
    nrm = din("nrm", [4, 1024])
    w_in = din("w_in", [1024, 5664])
    w_qkp = din("w_qkp", [1024, 1024])
    conv_w = din("conv_w", [5, 1536])
    adt = din("adt", [2, 16])
    dn_norm = din("dn_norm", [64])
    tb2d = din("tb2", [128, 8 * 16 * 64])
    w_ao = din("w_ao", [512, 1024])
    w_do = din("w_do", [512, 1024])
    w_out = din("w_out", [1024, 1024])
    w_ff1 = din("w_ff1", [1024, 4096])
    w_ff2 = din("w_ff2", [4096, 1024])
    identd = din("ident", [128, 128])
    trid = din("tri", [2, 128, 128])
    negid = din("negi", [2, 128, 128])
    negsd = din("negs", [2, 128, 128])
    masks_np, mindex = _get_na_masks()
    NMT = masks_np.shape[1]
    nmd = din("nam", [128, NMT * 128])
    cosd = din("cos", [128, 1024])
    sind = din("sin", [128, 1024])

    Y = [dout("yc", [1024, 1024]), dout("yl", [1024, 1024])]
    newk = dout("newk", [4, 8, 256, 64])
    newv = dout("newv", [4, 8, 256, 64])
    news = dout("news", [4, 2, 8, 64, 64])
    if os.environ.get("KPAD", "0") == "1":
        padt = [dout("padt%d" % i, [1024, 1024]) for i in range(4)]

    kb.start()
    es0 = kb.es

    for _e in ("pe", "act", "dve", "pool"):
        for _i in range(int(os.environ.get("KNOP_" + _e, os.environ.get("KNOP", "0")))):
            kb.E[_e].obj.nop()

    def ck(stage):
        if STAGE == stage and not kb.stopped:
            kb.barrier()
            kb.stopped = True
    mm, tr, act, tt, ts, stt, cp, dma = kb.mm, kb.tr, kb.act, kb.tt, kb.ts, kb.stt, kb.cp, kb.dma

    try:
        ident_f = kb.sb(es0, "ident_f", [128, 128], F32)
        ident_b = kb.sb(es0, "ident_b", [128, 128], BF16)
        ones_f = kb.sb(es0, "ones_f", [128, 128], F32)
        blk1 = kb.sb(es0, "blk1", [128, 128], BF16)
        modc = kb.sb(es0, "modc", [128, 48, 2], F32)
        nrmc = kb.sb(es0, "nrmc", [128, 4, 8], F32)
        A1 = kb.sb(es0, "A1", [128, 2, 8], F32)
        A2 = kb.sb(es0, "A2", [128, 2, 8], F32)
        Gc = kb.sb(es0, "Gc", [128, 2, 2, 8], F32)
        Grow = kb.sb(es0, "Grow", [128, 2, 2, 1024], F32, keys=[(p, w) for p in range(2) for w in range(2)])
        tri_s = kb.sb(es0, "tri_s", [128, 2, 128], F32)
        negi_s = kb.sb(es0, "negi_s", [128, 2, 128], F32)
        negs_s = kb.sb(es0, "negs_s", [128, 2, 128], F32)
        tri_b = kb.sb(es0, "tri_b", [128, 2, 128], BF16)
        ones_b = kb.sb(es0, "ones_b", [128, 128], BF16)
        dnw = kb.sb(es0, "dnw", [128, 64], F32)
        adt_s = kb.sb(es0, "adt_s", [128, 2, 16], F32)
        nea = kb.sb(es0, "nea", [128, 16], F32)
        convc = kb.sb(es0, "convc", [128, 5, 12], F32)

        dma("sp", ident_f[:], identd, [], [ident_f])
        dma("pool", ident_b[:], identd, [], [ident_b])
        kb.memset("dve", ones_f[:], 1.0, [ones_f])
        kb.memset("pool", blk1[:], 0.0, [blk1])
        kb.memset("pool", blk1[0:64, 0:64], 1.0, [blk1])
        kb.memset("pool", blk1[64:128, 64:128], 1.0, [blk1])
        dma("sp", nrmc[:], nrm.rearrange("w (c p) -> p w c", p=128), [], [nrmc], slow=True)
        dma("sp", tri_s[:], trid.rearrange("v p n -> p v n"), [], [tri_s])
        dma("sp", negi_s[:], negid.rearrange("v p n -> p v n"), [], [negi_s])
        dma("sp", negs_s[:], negsd.rearrange("v p n -> p v n"), [], [negs_s])
        cp("dve", tri_b[:], tri_s[:], [tri_s], [tri_b])
        kb.memset("dve", ones_b[:], 1.0, [ones_b])
        dma("sp", dnw[:], dn_norm.partition_broadcast(128), [], [dnw])
        dma("sp", adt_s[:], adt.partition_broadcast(128), [], [adt_s])
        for j in range(5):
            dma("sp", convc[:, j, :], conv_w[j].rearrange("(c p) -> p c", p=128), [], [convc], slow=True)
        act(nea[:], adt_s[:, 0, :], AF.Exp, [adt_s], [nea])
        ts("dve", nea[:], nea[:], -1.0, None, ALU.mult, None, [nea], [nea])
        with contextlib.ExitStack() as es:
            c2 = kb.sb(es, "c2", [128, 2, 8], F32)
            sc2 = kb.sb(es, "sc2", [128, 2, 8], F32)
            bad = kb.sb(es, "bad", [128, 48], F32)
            wa = [kb.sb(es, "wa%d" % i, [128, 8, 768], F32) for i in range(2)]
            for r_ in range(2):
                dma("sp", c2[:, r_, :], cvec[r_].rearrange("(c p) -> p c", p=128), [], [c2], slow=True)
            dma("sp", bad[:], b_ada.rearrange("(m p) -> p m", p=128), [], [bad], slow=True)
            act(sc2[:], c2[:], AF.Silu, [c2], [sc2])
            pmod, pmd = kb.bank(7)
            for blk in range(8):
                wb = wa[blk % 2]
                dma("sp", wb[:], w_ada[:, blk * 768:(blk + 1) * 768].rearrange("(kc p) n -> p kc n", p=128), [], [wb])
                for mi in range(6):
                    m = blk * 6 + mi
                    for kc in range(8):
                        mm(pmod[:, m * 2:m * 2 + 2], wb[:, kc, mi * 128:(mi + 1) * 128], sc2[:, :, kc], kc == 0, kc == 7,
                           [wb, sc2], [pmd])
            tt("dve", modc[:], pmod[:, 0:96].rearrange("p (m r) -> p m r", r=2), bad[:].unsqueeze(2).to_broadcast([128, 48, 2]),
               ALU.add, [pmd, bad], [modc])
            for p in range(2):
                stt(A1[:, p, :], modc[:, 8:16, p], 1.0, nrmc[:, 0, :], ALU.add, ALU.mult, [modc, nrmc], [A1])
                stt(A2[:, p, :], modc[:, 32:40, p], 1.0, nrmc[:, 2, :], ALU.add, ALU.mult, [modc, nrmc], [A2])
                tt("dve", Gc[:, p, 0, :], modc[:, 16:24, p], nrmc[:, 1, :], ALU.mult, [modc, nrmc], [Gc])
                tt("dve", Gc[:, p, 1, :], modc[:, 40:48, p], nrmc[:, 3, :], ALU.mult, [modc, nrmc], [Gc])
            dg = kb.sb(es, "dg", [128, 128], F32)
            for p in range(2):
                for w in range(2):
                    for c in range(8):
                        ts("dve", dg[:], ident_f[:], Gc[:, p, w, c:c + 1], None, ALU.mult, None, [ident_f, Gc], [dg])
                        pb, pbd = kb.bank()
                        mm(pb[:, 0:128], ones_f[:], dg[:], True, True, [ones_f, dg], [pbd])
                        cp("act", Grow[:, p, w, c * 128:(c + 1) * 128], pb[:, 0:128], [pbd], Grow.D((p, w)))
            kb.barrier()
            ck(0)

        for ps in range(2):
            lat = (ps == 1)
            if os.environ.get("KSKIP", "") == str(ps):
                continue
            T = 1024 if lat else 256
            nseq = 1 if lat else 4
            nch = T // 64
            with contextlib.ExitStack() as esP:
                xs = kb.sb(esP, "xs", [128, 8, 1024], F32, keys=list(range(8)))
                hT = kb.sb(esP, "hT", [128, 8, 1024], BF16, keys=list(range(8)))
                oT = kb.sb(esP, "oT", [128, 2, 4, 1024], BF16, keys=[(w, t) for w in range(2) for t in range(8)])
                rst = kb.sb(esP, "rst", [128, 8], F32, keys=list(range(8)))
                ssq = kb.sb(esP, "ssq", [128, 8], F32, keys=list(range(8)))
                xn = [kb.sb(esP, "xn%d" % i, [128, 1024], BF16) for i in range(2)]
                junk = kb.sb(esP, "junk", [128, 1024], BF16)

                def norm_to_T(t, Acol, dstT):
                    act(junk[:], xs[:, t, :], AF.Square, xs.D(t), [junk] + ssq.D(t), accum=ssq[:, t:t + 1])
                    act(rst[:, t:t + 1], ssq[:, t:t + 1], AF.Sqrt, ssq.D(t), rst.D(t), scale=1.0 / 1024.0, bias=EPS)
                    kb.recip(rst[:, t:t + 1], rst[:, t:t + 1], rst.D(t), rst.D(t))
                    xb = xn[t % 2]
                    ts("dve", xb[:], xs[:, t, :], rst[:, t:t + 1], None, ALU.mult, None, xs.D(t) + rst.D(t), [xb])
                    pb, pbd = kb.bank()
                    pbb = pb.bitcast(BF16)
                    for c in range(8):
                        tr(pbb[:, c * 128:(c + 1) * 128], xb[:, c * 128:(c + 1) * 128], ident_b[:], [xb, ident_b], [pbd])
                    A, Bc = Acol
                    for c in range(8):
                        if c % 2 == 0:
                            act(dstT[:, c, t * 128:(t + 1) * 128], pbb[:, c * 128:(c + 1) * 128], AF.Identity, [pbd, A1, A2, modc],
                                dstT.D(t), scale=A[:, c:c + 1], bias=Bc[:, c:c + 1])
                        else:
                            ts("dve", dstT[:, c, t * 128:(t + 1) * 128], pbb[:, c * 128:(c + 1) * 128], A[:, c:c + 1], Bc[:, c:c + 1],
                               ALU.mult, ALU.add, [pbd, A1, A2, modc], dstT.D(t))

                for t in range(8):
                    dma("sp", xs[:, t, :], X[ps][t * 128:(t + 1) * 128, :], [], xs.D(t))
                for t in range(8):
                    norm_to_T(t, (A1[:, ps, :], modc[:, 0:8, ps]), hT)
                kb.dump("hT%d" % ps, hT[:], [hT])
                ck(1 + 10 * ps)

                with contextlib.ExitStack() as esA:
                    dqT = kb.sb(esA, "dqT", [128, 4, 1024], BF16)
                    dkT = kb.sb(esA, "dkT", [128, 4, 1024], BF16)
                    k_tok = kb.sb(esA, "k_tok", [128, 8, 512], BF16)
                    v_tok = kb.sb(esA, "v_tok", [128, 8, 512], BF16)
                    zs = kb.sb(esA, "zs", [128, 8, 512], BF16)
                    gb = kb.sb(esA, "gb", [128, 8, 32], F32)
                    with contextlib.ExitStack() as esA1:
                        qT = kb.sb(esA1, "qT", [128, 4, 1024], BF16)
                        kT = kb.sb(esA1, "kT", [128, 4, 1024], BF16)
                        v_na = kb.sb(esA1, "v_na", [128, 8, 8, 66], BF16)
                        with contextlib.ExitStack() as es:
                            wbuf = [kb.sb(es, "wbuf%d" % i, [128, 8, 512], BF16) for i in range(2)]
                            wi = [0]

                            plan = []
                            for blk_ in range(2):
                                plan.append((w_in[:, blk_ * 512:(blk_ + 1) * 512], 512, False))
                                if lat:
                                    plan.append((w_qkp[:, blk_ * 512:(blk_ + 1) * 512], 512, True))
                            plan.append((w_in[:, 1024:1536], 512, False))
                            if not lat:
                                plan.append((w_in[:, 512:1024], 512, False))
                            plan.append((w_in[:, 3104:3616], 512, False))
                            plan.append((w_in[:, 3072:3104], 32, False))
                            for blk_ in range(3):
                                plan.append((w_in[:, 1536 + blk_ * 512:1536 + (blk_ + 1) * 512], 512, False))
                            issued = {}

                            def _issue(i):
                                if i in issued or i >= len(plan):
                                    return
                                src_, nc_, _ = plan[i]
                                b = wbuf[i % 2]
                                dma("pool", b[:, :, 0:nc_], src_.rearrange("(kc p) n -> p kc n", p=128), [], [b])
                                issued[i] = b

                            def load_w(src, ncols):
                                i = wi[0]
                                wi[0] += 1
                                assert plan[i][1] == ncols
                                _issue(i)
                                if not plan[i][2]:
                                    _issue(i + 1)
                                return issued[i]

                            def fm_group(wb, j, g, extra_R=()):
                                pb, pbd = kb.bank()
                                for kc in range(8):
                                    mm(pb, wb[:, kc, j * 128:(j + 1) * 128], hT[:, kc, g * 512:(g + 1) * 512], kc == 0, kc == 7,
                                       [wb] + hT.D(*range(4 * g, 4 * g + 4)), [pbd])
                                return pb, pbd

                            def tm_group(wb, t, ncols=512):
                                pb, pbd = kb.bank()
                                for kc in range(8):
                                    mm(pb[:, 0:ncols], hT[:, kc, t * 128:(t + 1) * 128], wb[:, kc, 0:ncols], kc == 0, kc == 7,
                                       [wb] + hT.D(t), [pbd])
                                return pb, pbd

                            if lat:
                                cos_s = kb.sb(es, "cos_s", [128, 1024], F32)
                                sin_s = kb.sb(es, "sin_s", [128, 1024], F32)
                                dma("sp", cos_s[:], cosd, [], [cos_s])
                                dma("sp", sin_s[:], sind, [], [sin_s])
                                rt = [kb.sb(es, "rt%d" % i, [128, 512], F32) for i in range(2)]
                            for blk in range(2):
                                wb = load_w(w_in[:, blk * 512:(blk + 1) * 512], 512)
                                dst = qT if blk == 0 else kT
                                if lat:
                                    wp = load_w(w_qkp[:, blk * 512:(blk + 1) * 512], 512)
                                for j in range(4):
                                    for g in range(2):
                                        pb, pbd = fm_group(wb, j, g)
                                        if not lat:
                                            cp("act", dst[:, j, g * 512:(g + 1) * 512], pb, [pbd], [dst])
                                        else:
                                            pb2, pbd2 = fm_group(wp, j, g)
                                            tt("dve", rt[0][:], pb, cos_s[:, g * 512:(g + 1) * 512], ALU.mult, [pbd, cos_s], [rt[0]])
                                            tt("dve", rt[1][:], pb2, sin_s[:, g * 512:(g + 1) * 512], ALU.mult, [pbd2, sin_s], [rt[1]])
                                            tt("pool", dst[:, j, g * 512:(g + 1) * 512], rt[0][:], rt[1][:], ALU.add, [rt[0], rt[1]], [dst])
                            ck(21 + 100 * ps)
                            kb.memset("pool", v_na[:], 1.0, [v_na])
                            stg = [kb.sb(es, "stg%d" % i, [128, 512], F32) for i in range(2)]
                            wv = load_w(w_in[:, 1024:1536], 512)
                            for t in range(8):
                                pb, pbd = tm_group(wv, t)
                                cp("dve", v_na[:, t, :, 0:64], pb.rearrange("p (h d) -> p h d", d=64), [pbd], [v_na])
                                if not lat:
                                    sg = stg[t % 2]
                                    cp("dve", sg[:], pb, [pbd], [sg])
                                    sq, half = t // 2, t % 2
                                    dma("sp", newv[sq, :, half * 128:(half + 1) * 128, :].rearrange("h s d -> s h d"),
                                        sg[:].rearrange("p (h d) -> p h d", d=64), [sg], [])
                            if not lat:
                                wk = load_w(w_in[:, 512:1024], 512)
                                for t in range(8):
                                    pb, pbd = tm_group(wk, t)
                                    sg = stg[t % 2]
                                    cp("dve", sg[:], pb, [pbd], [sg])
                                    sq, half = t // 2, t % 2
                                    dma("sp", newk[sq, :, half * 128:(half + 1) * 128, :].rearrange("h s d -> s h d"),
                                        sg[:].rearrange("p (h d) -> p h d", d=64), [sg], [])
                            ck(22 + 100 * ps)
                            wz = load_w(w_in[:, 3104:3616], 512)
                            for t in range(8):
                                pb, pbd = tm_group(wz, t)
                                act(zs[:, t, :], pb, AF.Silu, [pbd], [zs])
                            ck(23 + 100 * ps)
                            wba = load_w(w_in[:, 3072:3104], 32)
                            glog = kb.sb(es, "glog", [128, 8, 16], F32)
                            for t in range(8):
                                pb, pbd = tm_group(wba, t, 32)
                                act(gb[:, t, 0:16], pb[:, 0:16], AF.Sigmoid, [pbd], [gb])
                                tt("dve", glog[:, t, :], pb[:, 16:32], adt_s[:, 1, :], ALU.add, [pbd, adt_s], [glog])
                            act(glog[:], glog[:], AF.Exp, [glog], [glog])
                            act(glog[:], glog[:], AF.Ln, [glog], [glog], bias=1.0)
                            tt("dve", gb[:, :, 16:32], glog[:], nea[:].unsqueeze(1).to_broadcast([128, 8, 16]), ALU.mult,
                               [glog, nea], [gb])
                            ck(24 + 100 * ps)
                            cpre = kb.sb(es, "cpre", [128, nseq, T + 4], BF16)
                            kb.memset("pool", cpre[:], 0.0, [cpre])
                            sil = [kb.sb(es, "sil%d" % i, [128, 512], F32) for i in range(2)]
                            sqb = [kb.sb(es, "sqb%d" % i, [128, 512], BF16) for i in range(2)]
                            rno = [kb.sb(es, "rno%d" % i, [128, 512], F32) for i in range(2)]
                            vfm = kb.sb(es, "vfm", [128, 1024], BF16)
                            kfm_i = [0]
                            convd = [kb.sb(es, "convd%d" % i, [128, 5, 128], BF16) for i in range(2)]
                            for blk in range(3):
                                wb = load_w(w_in[:, 1536 + blk * 512:1536 + (blk + 1) * 512], 512)
                                for j in range(4):
                                    c = blk * 4 + j
                                    cdg = convd[c % 2]
                                    for jj in range(5):
                                        ts("dve" if jj % 2 else "pool", cdg[:, jj, :], ident_f[:], convc[:, jj, c:c + 1], None, ALU.mult, None,
                                           [ident_f, convc], [cdg])
                                    for g in range(2):
                                        pb, pbd = fm_group(wb, j, g)
                                        if lat:
                                            cp("act", cpre[:, 0, 2 + g * 512:2 + (g + 1) * 512], pb, [pbd], [cpre])
                                        else:
                                            cp("act", cpre[:, 2 * g:2 * g + 2, 2:2 + 256], pb.rearrange("p (s t) -> p s t", s=2), [pbd], [cpre])
                                    for g in range(2):
                                        pc, pcd = kb.bank()
                                        for jj in range(5):
                                            if lat:
                                                rhs = cpre[:, 0, g * 512 + jj:g * 512 + jj + 512]
                                            else:
                                                rhs = cpre[:, 2 * g:2 * g + 2, jj:jj + 256]
                                            mm(pc, cdg[:, jj, :], rhs, jj == 0, jj == 4, [cdg, cpre], [pcd])
                                        i2 = kfm_i[0] % 2
                                        kfm_i[0] += 1
                                        gs = slice(g * 512, (g + 1) * 512)
                                        if blk == 2:
                                            act(vfm[:, gs], pc, AF.Silu, [pcd], [vfm])
                                        else:
                                            s_ = sil[i2]
                                            act(s_[:], pc, AF.Silu, [pcd], [s_])
                                            tt("pool", sqb[i2][:], s_[:], s_[:], ALU.mult, [s_], [sqb[i2]])
                                            pn, pnd = kb.bank()
                                            mm(pn, blk1[:], sqb[i2][:], True, True, [blk1, sqb[i2]], [pnd])
                                            act(rno[i2][:], pn, AF.Sqrt, [pnd], [rno[i2]], bias=EPS)
                                            kb.recip(rno[i2][:], rno[i2][:], [rno[i2]], [rno[i2]])
                                            dst = dqT if blk == 0 else dkT
                                            if blk == 0:
                                                stt(dst[:, j, gs], s_[:], 0.125, rno[i2][:], ALU.mult, ALU.mult, [s_, rno[i2]], [dst])
                                            else:
                                                tt("dve", dst[:, j, gs], s_[:], rno[i2][:], ALU.mult, [s_, rno[i2]], [dst])
                                    if blk >= 1:
                                        src = dkT[:, j, :] if blk == 1 else vfm[:]
                                        srcd = dkT if blk == 1 else vfm
                                        dstb = k_tok if blk == 1 else v_tok
                                        for t in range(8):
                                            pb, pbd = kb.bank()
                                            pbb = pb.bitcast(BF16)
                                            tr(pbb[:, 0:128], src[:, t * 128:(t + 1) * 128], ident_b[:], [srcd, ident_b], [pbd])
                                            cp("act" if t % 2 else "dve", dstb[:, t, j * 128:(j + 1) * 128], pbb[:, 0:128], [pbd], [dstb])
                            kb.dump("qT%d" % ps, qT[:], [qT])
                            kb.dump("kT%d" % ps, kT[:], [kT])
                            kb.dump("dqT%d" % ps, dqT[:], [dqT])
                            kb.dump("dkT%d" % ps, dkT[:], [dkT])
                            kb.dump("vtok%d" % ps, v_tok[:], [v_tok])
                            kb.dump("gb%d" % ps, gb[:], [gb])
                            kb.dump("zs%d" % ps, zs[:], [zs])
                            kb.barrier()
                            ck(2 + 10 * ps)

                        with contextlib.ExitStack() as es:
                            pt = [kb.sb(es, "pt%d" % i, [128, 7, 128], BF16) for i in range(2)]
                            ona = kb.sb(es, "ona", [128, 512], BF16)
                            rsum = kb.sb(es, "rsum", [128, 8], F32)
                            if lat:
                                tb2 = kb.sb(es, "tb2", [128, 8, 16, 64], BF16)
                                nam = kb.sb(es, "nam", [128, NMT, 128], BF16)
                                ckt = kb.sb(es, "ckt", [128, 2, 512], BF16)
                                ckT = kb.sb(es, "ckT", [128, 2, 4, 256], BF16)
                                kTz = kb.sb(es, "kTz", [128, 4, 1024], BF16)
                                kb.memset("pool", kTz[0:64, :, :], 0.0, [kTz])
                                kb.memset("pool", ckT[0:64, 1, :, :], 0.0, [ckT])
                                cp("pool", kTz[64:128, :, :], kT[64:128, :, :], [kT], [kTz])
                                cva = kb.sb(es, "cva", [128, 2, 8, 66], BF16)
                                dma("pool", tb2[:].rearrange("p h m q -> p (h m q)"), tb2d, [], [tb2])
                                dma("pool", nam[:].rearrange("p t q -> p (t q)"), nmd, [], [nam])
                                kb.memset("dve", cva[:], 1.0, [cva])
                                for kt2 in range(2):
                                    dma("pool", ckt[:, kt2, :].rearrange("p (h d) -> p h d", d=64),
                                        ckd[:, kt2 * 128:(kt2 + 1) * 128, :].rearrange("h s d -> s h d"), [], [ckt])
                                    dma("pool", cva[:, kt2, :, 0:64],
                                        cvd[:, kt2 * 128:(kt2 + 1) * 128, :].rearrange("h s d -> s h d"), [], [cva])
                                ts("dve", tb2[:], tb2[:], 8.0, None, ALU.mult, None, [tb2], [tb2])
                                for kt2 in range(2):
                                    for pr in range(4):
                                        pb, pbd = kb.bank(6 + (pr % 2))
                                        pbb = pb.bitcast(BF16)
                                        tr(pbb[:, 0:128], ckt[:, kt2, pr * 128:(pr + 1) * 128], ident_b[:], [ckt, ident_b], [pbd])
                                        cp("dve", ckT[:, 0, pr, kt2 * 128:(kt2 + 1) * 128], pbb[:, 0:128], [pbd], [ckT])
                                        cp("dve", ckT[64:128, 1, pr, kt2 * 128:(kt2 + 1) * 128], pbb[64:128, 0:128], [pbd], [ckT])
                            n_qt = 8
                            for qi in range(n_qt):
                                if lat:
                                    ltiles = na_tiles(qi)
                                    ktl = [("l", kt) for kt in ltiles] + [("c", 0), ("c", 1)]
                                else:
                                    sq = qi // 2
                                    ktl = [("l", 2 * sq), ("l", 2 * sq + 1)]
                                nk = len(ktl)
                                pvb = [kb.bank(4), kb.bank(5)]
                                for h in range(8):
                                    pr, hb = h // 2, (h % 2) * 64
                                    par = (qi * 8 + h) % 2
                                    sA, sAd = kb.bank(2 * par)
                                    sB, sBd = kb.bank(2 * par + 1)
                                    P_ = pt[par]
                                    zp = lat and (h % 2 == 1)
                                    qap = qT[:, pr, qi * 128:(qi + 1) * 128] if zp else qT[hb:hb + 64, pr, qi * 128:(qi + 1) * 128]
                                    for idx, (kind, kt) in enumerate(ktl):
                                        sap, sd = (sA, sAd) if idx < 4 else (sB, sBd)
                                        o_ = sap[:, (idx % 4) * 128:(idx % 4 + 1) * 128]
                                        if kind == "l":
                                            kap = kTz[:, pr, kt * 128:(kt + 1) * 128] if zp else kT[hb:hb + 64, pr, kt * 128:(kt + 1) * 128]
                                            if lat:
                                                m0 = 7 - (2 * kt - 2 * qi)
                                                mm(o_, kap, qap, True, False, [kTz, kT, qT], [sd])
                                                mm(o_, ident_b[:], tb2[:, h, m0:m0 + 2, :], False, False, [ident_b, tb2], [sd])
                                                mm(o_, ident_b[:], nam[:, mindex[(qi, kt)], :], False, True, [ident_b, nam], [sd])
                                            else:
                                                mm(o_, kap, qap, True, True, [kT, qT], [sd])
                                        else:
                                            mm(o_, ckT[:, 1, pr, kt * 128:(kt + 1) * 128] if zp else ckT[0:64, 0, pr, kt * 128:(kt + 1) * 128], qap, True, True, [ckT, qT], [sd])
                                    n1 = min(nk, 4)
                                    act(P_[:, 0:n1, :], sA[:, 0:n1 * 128].rearrange("p (k q) -> p k q", q=128), AF.Exp, [sAd], [P_], scale=0.125)
                                    if nk > 4:
                                        act(P_[:, 4:nk, :], sB[:, 0:(nk - 4) * 128].rearrange("p (k q) -> p k q", q=128), AF.Exp, [sBd], [P_], scale=0.125)
                                    pv, pvd = pvb[h // 4]
                                    o_ = pv[:, (h % 4) * 65:(h % 4) * 65 + 65]
                                    for idx, (kind, kt) in enumerate(ktl):
                                        if kind == "l":
                                            vap, vd = v_na[:, kt, h, 0:65], v_na
                                        else:
                                            vap, vd = cva[:, kt, h, 0:65], cva
                                        mm(o_, P_[:, idx, :], vap, idx == 0, idx == nk - 1, [P_, vd], [pvd])
                                for hg in range(2):
                                    pv, pvd = pvb[hg]
                                    pv3 = pv[:, 0:260].rearrange("p (h e) -> p h e", e=65)
                                    kb.recip(rsum[:, hg * 4:hg * 4 + 4], pv3[:, :, 64], [pvd], [rsum])
                                    tt("dve", ona[:, hg * 256:(hg + 1) * 256].rearrange("p (h d) -> p h d", d=64), pv3[:, :, 0:64],
                                       rsum[:, hg * 4:hg * 4 + 4].unsqueeze(2).to_broadcast([128, 4, 64]), ALU.mult, [pvd, rsum], [ona])
                                for c in range(4):
                                    pb, pbd = kb.bank(6 + (c % 2))
                                    pbb = pb.bitcast(BF16)
                                    tr(pbb[:, 0:128], ona[:, c * 128:(c + 1) * 128], ident_b[:], [ona, ident_b], [pbd])
                                    cp("act", oT[:, 0, c, qi * 128:(qi + 1) * 128], pbb[:, 0:128], [pbd], oT.D((0, qi)))
                            kb.dump("onaT%d" % ps, oT[:, 0, :, :], [oT])
                            kb.barrier()
                            ck(3 + 10 * ps)
                    with contextlib.ExitStack() as es:
                        o_dn = kb.sb(es, "o_dn", [128, 8, 512], F32, keys=list(range(8)))
                        g_st = kb.sb(es, "g_st", [128, 8], F32)
                        b_st = kb.sb(es, "b_st", [128, 8], F32)
                        gc_st = kb.sb(es, "gc_st", [128, 8], F32)
                        ghl = kb.sb(es, "ghl", [128, 2, 8], BF16)
                        egc = kb.sb(es, "egc", [128, 8], F32)
                        ekd = kb.sb(es, "ekd", [128, 8], F32)
                        egl = kb.sb(es, "egl", [128, 2, 8], F32)
                        DG = kb.sb(es, "DG", [128, 8, 128], F32)
                        DF = kb.sb(es, "DF", [128, 8, 128], F32)
                        DTi = kb.sb(es, "DTi", [128, 8, 128], BF16)
                        DTs = kb.sb(es, "DTs", [128, 8, 128], BF16)
                        Pk = [kb.sb(es, "Pk%d" % i, [128, 8, 128], F32, keys=[0, 1, 2, 3]) for i in range(2)]
                        PkT = [kb.sb(es, "PkT%d" % i, [128, 8, 128], F32, keys=[0, 1, 2, 3]) for i in range(2)]
                        Yk = [kb.sb(es, "Yk%d" % i, [128, 8, 128], F32, keys=[0, 1, 2, 3]) for i in range(2)]
                        tM = Pk[0]
                        identf8 = ident_f[:].unsqueeze(1).to_broadcast([128, 8, 128])
                        QKD = kb.sb(es, "QKD", [128, 8, 128], BF16)
                        kq = kb.sb(es, "kq", [128, 4, 256], BF16)
                        ksg = kq
                        qsg = kq
                        kdst = kb.sb(es, "kdst", [128, 8, 64], BF16)
                        vst = kb.sb(es, "vst", [128, 8, 64], BF16)
                        t1 = kb.sb(es, "t1", [128, 8, 64], F32)
                        t2 = kb.sb(es, "t2", [128, 8, 64], F32)
                        rr = kb.sb(es, "rr", [128, 8, 64], BF16)
                        vnew = kb.sb(es, "vnew", [128, 8, 64], BF16)
                        Sf = kb.sb(es, "Sf", [128, 4, 2, 64], F32)
                        Sb = kb.sb(es, "Sb", [128, 4, 2, 64], BF16)
                        for t in range(8):
                            kb.memset("pool", o_dn[:, t, :], 0.0, o_dn.D(t))
                        PAd, PBd, PCd, PDd = kb.PA.D(), kb.PB.D(), kb.PC.D(), kb.PD.D()
                        PA8 = kb.PA[:, :].rearrange("p (h n) -> p h n", h=8)
                        PB8 = kb.PB[:, :].rearrange("p (h n) -> p h n", h=8)
                        PC8 = kb.PC[:, :].rearrange("p (h n) -> p h n", h=8)
                        PD8 = kb.PD[:, :].rearrange("p (h n) -> p h n", h=8)
                        identb8 = ident_b[:].unsqueeze(1).to_broadcast([128, 8, 128])
                        for sq in range(nseq):
                            tb = sq * T
                            if lat:
                                for dr in range(2):
                                    for par in range(2):
                                        dma("sp", Sf[par * 64:(par + 1) * 64, :, dr, :],
                                            std[dr, :, :, :].rearrange("(pr two) k v -> two k pr v", two=2)[par], [], [Sf])
                            else:
                                kb.memset("dve", Sf[:], 0.0, [Sf])
                            cp("pool", Sb[:], Sf[:], [Sf], [Sb])
                            for s in range(nch):

                                cf, cb = s, nch - 1 - s
                                hf = cf % 2
                                hbk = 1 - hf
                                ch = [None, None]
                                ch[hf], ch[hbk] = cf, cb
                                dirh = [None, None]
                                dirh[hf], dirh[hbk] = 0, 1
                                tok = [tb + ch[0] * 64, tb + ch[1] * 64]
                                til = [tok[0] // 128, tok[1] // 128]
                                hs = [slice(0, 64), slice(64, 128)]
                                SL = [(h % 2) * 4 + h // 2 for h in range(8)]
                                for x in range(2):
                                    d_ = dirh[x]
                                    cp("pool", g_st[hs[x], :].rearrange("p (two pr) -> p two pr", two=2),
                                       gb[hs[x], til[x], 16 + d_ * 8:24 + d_ * 8].rearrange("p (pr two) -> p two pr", two=2), [gb], [g_st])
                                    cp("pool", b_st[hs[x], :].rearrange("p (two pr) -> p two pr", two=2),
                                       gb[hs[x], til[x], d_ * 8:d_ * 8 + 8].rearrange("p (pr two) -> p two pr", two=2), [gb], [b_st])
                                    cp("pool", kq[:, :, x * 64:(x + 1) * 64], dkT[:, :, tok[x]:tok[x] + 64], [dkT], [kq])
                                    cp("pool", kq[:, :, 128 + x * 64:128 + (x + 1) * 64], dqT[:, :, tok[x]:tok[x] + 64], [dqT], [kq])
                                    cp("pool", vst[hs[x], :, :].rearrange("p (two pr) d -> p two pr d", two=2),
                                       v_tok[hs[x], til[x], :].rearrange("p (pr two d) -> p two pr d", two=2, d=64), [v_tok], [vst])
                                cp("pool", ghl[:, 0, :], g_st[:], [g_st], [ghl])
                                tt("dve", ghl[:, 1, :], g_st[:], ghl[:, 0, :], ALU.subtract, [g_st, ghl], [ghl])
                                mm(kb.PD[:, 0:8], tri_b[:, hf, :], ghl[:, 0, :], True, False, [tri_b, ghl], PDd)
                                mm(kb.PD[:, 0:8], tri_b[:, hf, :], ghl[:, 1, :], False, True, [tri_b, ghl], PDd)
                                cp("dve", gc_st[:], kb.PD[:, 0:8], PDd, [gc_st])
                                act(egc[:], gc_st[:], AF.Exp, [gc_st], [egc])
                                ck(41 + 1000 * ps)
                                cp("pool", ghl[:, 0, :], gc_st[:], [gc_st], [ghl])
                                tt("dve", ghl[:, 1, :], gc_st[:], ghl[:, 0, :], ALU.subtract, [gc_st, ghl], [ghl])
                                DGh, DGl = Pk[1], PkT[1]
                                DGh_ap = Pk[1][:].rearrange("p h n -> p (h n)").bitcast(BF16)[:, 0:1024].rearrange("p (h n) -> p h n", h=8)
                                DGl_ap = PkT[1][:].rearrange("p h n -> p (h n)").bitcast(BF16)[:, 0:1024].rearrange("p (h n) -> p h n", h=8)
                                tt("dve", DGh_ap, identb8, ghl[:, 0, :].unsqueeze(2).to_broadcast([128, 8, 128]), ALU.mult, [ident_b, ghl], [DGh])
                                tt("pool", DGl_ap, identb8, ghl[:, 1, :].unsqueeze(2).to_broadcast([128, 8, 128]), ALU.mult, [ident_b, ghl], [DGl])
                                for hh in range(2):
                                    mm(kb.PB[:, hh * 512:(hh + 1) * 512], ones_b[:], DGh_ap[:, hh * 4:(hh + 1) * 4, :].rearrange("p h n -> p (h n)"),
                                       True, False, [ones_b, DGh], PBd)
                                    mm(kb.PB[:, hh * 512:(hh + 1) * 512], ones_b[:], DGl_ap[:, hh * 4:(hh + 1) * 4, :].rearrange("p h n -> p (h n)"),
                                       False, True, [ones_b, DGl], PBd)
                                lastf, lastb = hf * 64 + 63, hbk * 64
                                act(egl[:, 0, :], PB8[:, :, lastf], AF.Exp, PBd, [egl])
                                act(egl[:, 1, :], PB8[:, :, lastb], AF.Exp, PBd, [egl])
                                tt("dve", ekd[hs[hf], :], PB8[hs[hf], :, lastf], gc_st[hs[hf], :], ALU.subtract, PBd + [gc_st], [ekd])
                                tt("dve", ekd[hs[hbk], :], PB8[hs[hbk], :, lastb], gc_st[hs[hbk], :], ALU.subtract, PBd + [gc_st], [ekd])
                                act(ekd[:], ekd[:], AF.Exp, [ekd], [ekd])
                                tt("dve", DF[:], PB8, gc_st[:].unsqueeze(2).to_broadcast([128, 8, 128]), ALU.subtract, PBd + [gc_st], [DF])
                                tt("pool", DG[:], DF[:], negi_s[:, hf, :].unsqueeze(1).to_broadcast([128, 8, 128]), ALU.add, [DF, negi_s], [DG])
                                tt("pool", DF[:], DF[:], negs_s[:, hf, :].unsqueeze(1).to_broadcast([128, 8, 128]), ALU.add, [DF, negs_s], [DF])
                                act(DTi[:], DG[:], AF.Exp, [DG], [DTi])
                                act(DTs[:], DF[:], AF.Exp, [DF], [DTs])
                                ck(42 + 1000 * ps)
                                for x in range(2):
                                    tt("dve", kdst[hs[x], :, :].rearrange("p (two pr) d -> p two pr d", two=2),
                                       k_tok[hs[x], til[x], :].rearrange("p (pr two d) -> p two pr d", two=2, d=64),
                                       ekd[hs[x], :].rearrange("p (two pr) -> p two pr", two=2).unsqueeze(3).to_broadcast([64, 2, 4, 64]), ALU.mult, [k_tok, ekd], [kdst])
                                PAq = [kb.PA.D(q) for q in range(4)]
                                PBq = [kb.PB.D(q // 2) for q in range(4)]
                                qs = [slice(2 * q, 2 * q + 2) for q in range(4)]
                                for h in range(8):
                                    pr, hb, sl = h // 2, (h % 2) * 64, SL[h]
                                    mm(PA8[:, sl, :], kq[hb:hb + 64, pr, 0:128], kq[hb:hb + 64, pr, :], True, True, [kq], PAq[sl // 2])
                                b8 = b_st[:].unsqueeze(2).to_broadcast([128, 8, 128])
                                for q in (0, 2, 1, 3):
                                    tt("dve", Pk[0][:, qs[q], :], PA8[:, qs[q], 0:128], DTs[:, qs[q], :], ALU.mult, PAq[q] + [DTs], Pk[0].D(q))
                                    tt("pool", Pk[0][:, qs[q], :], Pk[0][:, qs[q], :], b8[:, qs[q], :], ALU.mult, Pk[0].D(q) + [b_st], Pk[0].D(q))
                                    stt(Yk[0][:, qs[q], :], Pk[0][:, qs[q], :], -1.0, identf8[:, qs[q], :], ALU.mult, ALU.add,
                                        Pk[0].D(q) + [ident_f], Yk[0].D(q))
                                    for sl in (2 * q, 2 * q + 1):
                                        tr(PB8[:, sl, :], Pk[0][:, sl, :], ident_f[:], Pk[0].D(q) + [ident_f], PBq[q])
                                for q in (0, 2, 1, 3):
                                    tt("dve", QKD[:, qs[q], :], PA8[:, qs[q], 128:256], DTi[:, qs[q], :], ALU.mult, PAq[q] + [DTi], [QKD])
                                    cp("dve", PkT[0][:, qs[q], :], PB8[:, qs[q], :], PBq[q], PkT[0].D(q))
                                ck(44 + 1000 * ps)
                                PAq = [kb.PA.D(q) for q in range(4)]
                                PBq = [kb.PB.D(q // 2) for q in range(4)]
                                qs = [slice(2 * q, 2 * q + 2) for q in range(4)]
                                for q in range(4):
                                    for sl in (2 * q, 2 * q + 1):
                                        mm(PA8[:, sl, 128:256], PkT[0][:, sl, :], Pk[0][:, sl, :], True, True, PkT[0].D(q) + Pk[0].D(q), PAq[q])
                                        mm(PB8[:, sl, :], Pk[0][:, sl, :], PkT[0][:, sl, :], True, True, PkT[0].D(q) + Pk[0].D(q), PBq[q])
                                for q in range(4):
                                    cp("dve", Pk[1][:, qs[q], :], PA8[:, qs[q], 128:256], PAq[q], Pk[1].D(q))
                                    cp("dve", PkT[1][:, qs[q], :], PB8[:, qs[q], :], PBq[q], PkT[1].D(q))
                                ck(442 + 1000 * ps)
                                cur = 1
                                ycur = 0
                                for lev in range(1, 6):
                                    nxt = 1 - cur
                                    last = (lev == 5)
                                    for q in range(4):
                                        for sl in (2 * q, 2 * q + 1):
                                            mm(PA8[:, sl, 0:128], PkT[cur][:, sl, :], Yk[ycur][:, sl, :], True, True,
                                               PkT[cur].D(q) + Yk[ycur].D(q), PAq[q])
                                            if not last:
                                                mm(PA8[:, sl, 128:256], PkT[cur][:, sl, :], Pk[cur][:, sl, :], True, True,
                                                   PkT[cur].D(q) + Pk[cur].D(q), PAq[q])
                                                mm(PB8[:, sl, :], Pk[cur][:, sl, :], PkT[cur][:, sl, :], True, True,
                                                   PkT[cur].D(q) + Pk[cur].D(q), PBq[q])
                                    for q in range(4):
                                        tt("dve", Yk[1 - ycur][:, qs[q], :], PA8[:, qs[q], 0:128], Yk[ycur][:, qs[q], :], ALU.add,
                                           PAq[q] + Yk[ycur].D(q), Yk[1 - ycur].D(q))
                                        if not last:
                                            cp("dve", Pk[nxt][:, qs[q], :], PA8[:, qs[q], 128:256], PAq[q], Pk[nxt].D(q))
                                            cp("dve", PkT[nxt][:, qs[q], :], PB8[:, qs[q], :], PBq[q], PkT[nxt].D(q))
                                    ycur = 1 - ycur
                                    if not last:
                                        cur = nxt
                                cp("pool", DTs[:], Yk[ycur][:], [Yk[ycur]], [DTs])
                                CT = DTs
                                ck(45 + 1000 * ps)
                                KS = kb.PA[:, 0:1024].rearrange("p (b n) -> p b n", b=2)[:, :, 0:256].rearrange("p b (pr d) -> p b pr d", d=64)
                                QS = kb.PB[:, 0:1024].rearrange("p (b n) -> p b n", b=2)[:, :, 0:256].rearrange("p b (pr d) -> p b pr d", d=64)
                                VN = kb.PA[:, 1024:1536].rearrange("p (h d) -> p h d", d=64)
                                IT = kb.PA[:, 1536:2048].rearrange("p (h d) -> p h d", d=64)
                                egc4 = egc[:].rearrange("p (b pr) -> p b pr", b=2).unsqueeze(3).to_broadcast([128, 2, 4, 64])
                                v4 = lambda bf: bf[:].rearrange("p (b pr) d -> p b pr d", b=2)
                                for h in range(8):
                                    pr, hb, par = h // 2, (h % 2) * 64, h % 2
                                    for x in range(2):
                                        mm(KS[hs[x], par, pr, :], kq[hb:hb + 64, pr, x * 64:(x + 1) * 64], Sb[hb:hb + 64, pr, dirh[x], :],
                                           True, True, [ksg, Sb], PAd)
                                tt("dve", v4(t1), KS, egc4, ALU.mult, PAd + [egc], [t1])
                                tt("pool", rr[:], vst[:], t1[:], ALU.subtract, [vst, t1], [rr])
                                for sl in range(8):
                                    mm(VN[:, sl, :], CT[:, sl, :], rr[:, sl, :], True, True, [CT, rr], PAd)
                                tt("dve", vnew[:], VN, b_st[:].unsqueeze(2).to_broadcast([128, 8, 64]), ALU.mult, PAd + [b_st], [vnew])
                                ck(46 + 1000 * ps)
                                for h in range(8):
                                    pr, hb, par = h // 2, (h % 2) * 64, h % 2
                                    for x in range(2):
                                        mm(QS[hs[x], par, pr, :], kq[hb:hb + 64, pr, 128 + x * 64:128 + (x + 1) * 64], Sb[hb:hb + 64, pr, dirh[x], :],
                                           True, True, [qsg, Sb], PBd)
                                for sl in range(8):
                                    mm(IT[:, sl, :], QKD[:, sl, :], vnew[:, sl, :], True, True, [QKD, vnew], PAd)
                                tt("dve", v4(t1), QS, egc4, ALU.mult, PBd + [egc], [t1])
                                tt("dve", t2[:], IT, t1[:], ALU.add, PAd + [t1], [t2])
                                for x in range(2):
                                    tt("pool", o_dn[hs[x], til[x], :].rearrange("p (pr two d) -> p two pr d", two=2, d=64),
                                       o_dn[hs[x], til[x], :].rearrange("p (pr two d) -> p two pr d", two=2, d=64),
                                       t2[hs[x], :, :].rearrange("p (two pr) d -> p two pr d", two=2), ALU.add, [t2] + o_dn.D(til[x]), o_dn.D(til[x]))
                                ck(47 + 1000 * ps)
                                PCs = [kb.PC[:, 0:256].rearrange("p (pr v) -> p pr v", pr=4), kb.PD[:, 0:256].rearrange("p (pr v) -> p pr v", pr=4)]
                                PCsd = [PCd, PDd]
                                for h in range(8):
                                    pr, hb, sl = h // 2, (h % 2) * 64, SL[h]
                                    for x in range(2):
                                        mm(PCs[x][hb:hb + 64, pr, :], kdst[hs[x], sl, :], vnew[hs[x], sl, :], True, True, [kdst, vnew], PCsd[x])
                                for par in range(2):
                                    ps_ = slice(par * 64, par * 64 + 64)
                                    eg = egl[ps_, :, par * 4:(par + 1) * 4].rearrange("p dr pr -> p pr dr")
                                    tt("dve", Sf[ps_, :, :, :], Sf[ps_, :, :, :], eg.unsqueeze(3).to_broadcast([64, 4, 2, 64]), ALU.mult,
                                       [Sf, egl], [Sf])
                                for x in range(2):
                                    tt("dve", Sf[:, :, dirh[x], :], Sf[:, :, dirh[x], :], PCs[x], ALU.add, PCsd[x] + [Sf], [Sf])
                                cp("pool", Sb[:], Sf[:], [Sf], [Sb])
                                ck(20000 + 100 * ps + s)
                            if not lat:
                                seq_g = sq
                                for dr in range(2):
                                    for par in range(2):
                                        dma("sp", news[seq_g, dr, :, :, :].rearrange("(pr two) k v -> two k pr v", two=2)[par],
                                            Sf[par * 64:(par + 1) * 64, :, dr, :], [Sf], [])
                        kb.dump("odn%d" % ps, o_dn[:], [o_dn])
                        ors = kb.sb(es, "ors", [128, 8], F32)
                        osq_ap = tM[:, 0:4, :].rearrange("p h n -> p (h n)")
                        og_ap = DG[:, 0:4, :].rearrange("p h n -> p (h n)")
                        ogb_ap = QKD[:, 0:4, :].rearrange("p h n -> p (h n)")
                        for t in range(8):
                            tt("pool", osq_ap, o_dn[:, t, :], o_dn[:, t, :], ALU.mult, o_dn.D(t), [tM])
                            kb.reduce_sum(ors[:], osq_ap.rearrange("p (h d) -> p h d", d=64), [tM], [ors])
                            act(ors[:], ors[:], AF.Sqrt, [ors], [ors], scale=1.0 / 64.0, bias=EPS)
                            kb.recip(ors[:], ors[:], [ors], [ors])
                            tt("dve", og_ap.rearrange("p (h d) -> p h d", d=64), o_dn[:, t, :].rearrange("p (h d) -> p h d", d=64),
                               ors[:].unsqueeze(2).to_broadcast([128, 8, 64]), ALU.mult, o_dn.D(t) + [ors], [DG])
                            tt("pool", og_ap.rearrange("p (h d) -> p h d", d=64), og_ap.rearrange("p (h d) -> p h d", d=64),
                               dnw[:].unsqueeze(1).to_broadcast([128, 8, 64]), ALU.mult, [DG, dnw], [DG])
                            tt("dve", ogb_ap, og_ap, zs[:, t, :], ALU.mult, [DG, zs], [QKD])
                            for c in range(4):
                                pb, pbd = kb.bank()
                                pbb = pb.bitcast(BF16)
                                tr(pbb[:, 0:128], ogb_ap[:, c * 128:(c + 1) * 128], ident_b[:], [QKD, ident_b], [pbd])
                                cp("dve", oT[:, 1, c, t * 128:(t + 1) * 128], pbb[:, 0:128], [pbd], oT.D((1, t)))
                        kb.barrier()
                        ck(4 + 10 * ps)
                with contextlib.ExitStack() as es:
                    mT = kb.sb(es, "mT", [128, 8, 1024], BF16)
                    wao = kb.sb(es, "wao", [128, 4, 1024], BF16)
                    wdo = kb.sb(es, "wdo", [128, 4, 1024], BF16)
                    wo = kb.sb(es, "wo", [128, 8, 1024], BF16)
                    wg = [kb.sb(es, "wg%d" % i, [128, 8, 256], BF16) for i in range(2)]
                    sg_ = [kb.sb(es, "sg%d" % i, [128, 512], F32) for i in range(2)]
                    tm_ = [kb.sb(es, "tm%d" % i, [128, 512], F32) for i in range(2)]
                    dma("pool", wao[:], w_ao.rearrange("(kc p) n -> p kc n", p=128), [], [wao])
                    dma("pool", wdo[:], w_do.rearrange("(kc p) n -> p kc n", p=128), [], [wdo])
                    dma("pool", wo[:], w_out.rearrange("(kc p) n -> p kc n", p=128), [], [wo])
                    def issue_wg(m_):
                        if m_ >= 8:
                            return
                        wgb_ = wg[m_ % 2]
                        dma("pool", wgb_[:, :, 0:128], w_in[:, 3616 + m_ * 128:3616 + (m_ + 1) * 128].rearrange("(kc p) n -> p kc n", p=128), [], [wgb_])
                        dma("pool", wgb_[:, :, 128:256], w_in[:, 4640 + m_ * 128:4640 + (m_ + 1) * 128].rearrange("(kc p) n -> p kc n", p=128), [], [wgb_])

                    issue_wg(0)
                    for m in range(8):
                        wgb = wg[m % 2]
                        issue_wg(m + 1)
                        for g in range(2):
                            gs = slice(g * 512, (g + 1) * 512)
                            pg = [kb.bank(), kb.bank()]
                            for w in range(2):
                                for kc in range(8):
                                    mm(pg[w][0], wgb[:, kc, w * 128:(w + 1) * 128], hT[:, kc, gs], kc == 0, kc == 7, [wgb, hT], [pg[w][1]])
                                act(sg_[w][:], pg[w][0], AF.Sigmoid, [pg[w][1]], [sg_[w]])
                            pbr = [kb.bank(), kb.bank()]
                            for w in range(2):
                                wsrc = wao if w == 0 else wdo
                                for kc in range(4):
                                    mm(pbr[w][0], wsrc[:, kc, m * 128:(m + 1) * 128], oT[:, w, kc, gs], kc == 0, kc == 3, [wsrc, oT], [pbr[w][1]])
                                tt("dve", tm_[w][:], pbr[w][0], sg_[w][:], ALU.mult, [pbr[w][1], sg_[w]], [tm_[w]])
                            tt("pool", mT[:, m, gs], tm_[0][:], tm_[1][:], ALU.add, [tm_[0], tm_[1]], [mT])
                    kb.dump("mT%d" % ps, mT[:], [mT])

                    def post_norm(t, pyA, pyB, which):
                        (ya, yad), (yb, ybd) = pyA, pyB
                        s2 = kb.sb
                        act(junk[:, 0:512], ya, AF.Square, [yad], [junk] + ssq.D(t), accum=ssq[:, t:t + 1])
                        act(junk[:, 512:1024], yb, AF.Square, [ybd], [junk] + rst.D(t), accum=rst[:, t:t + 1])
                        tt("dve", ssq[:, t:t + 1], ssq[:, t:t + 1], rst[:, t:t + 1], ALU.add, ssq.D(t) + rst.D(t), ssq.D(t))
                        act(rst[:, t:t + 1], ssq[:, t:t + 1], AF.Sqrt, ssq.D(t), rst.D(t), scale=1.0 / 1024.0, bias=EPS)
                        kb.recip(rst[:, t:t + 1], rst[:, t:t + 1], rst.D(t), rst.D(t))
                        for hh, (yy, yd) in enumerate(((ya, yad), (yb, ybd))):
                            cs_ = slice(hh * 512, (hh + 1) * 512)
                            tmp = tm2[hh]
                            stt(tmp[:], yy, rst[:, t:t + 1], Grow[:, ps, which, cs_], ALU.mult, ALU.mult, [yd] + rst.D(t) + Grow.D((ps, which)), [tmp])
                            tt("pool", xs[:, t, cs_], xs[:, t, cs_], tmp[:], ALU.add, [tmp] + xs.D(t), xs.D(t))

                    tm2 = [kb.sb(es, "tm2%d" % i, [128, 512], F32) for i in range(2)]
                    for t in range(8):
                        py = [kb.bank(), kb.bank()]
                        for hh in range(2):
                            for kc in range(8):
                                mm(py[hh][0], mT[:, kc, t * 128:(t + 1) * 128], wo[:, kc, hh * 512:(hh + 1) * 512], kc == 0, kc == 7, [mT, wo], [py[hh][1]])
                        post_norm(t, py[0], py[1], 0)
                    kb.dump("x1_%d" % ps, xs[:], [xs])
                    kb.barrier()
                    ck(5 + 10 * ps)
                with contextlib.ExitStack() as es:
                    aT = kb.sb(es, "aT", [128, 16, 1024], BF16)
                    fTs = kb.sb(es, "fTs", [128, 8, 1024], F32)
                    w1b = [kb.sb(es, "w1b%d" % i, [128, 8, 512], BF16) for i in range(2)]
                    w2b = [kb.sb(es, "w2b%d" % i, [128, 16, 128], BF16) for i in range(2)]
                    rl = [kb.sb(es, "rl%d" % i, [128, 512], BF16) for i in range(2)]
                    tm2 = [kb.sb(es, "tm2f%d" % i, [128, 512], F32) for i in range(2)]
                    h2T = hT
                    for t in range(8):
                        norm_to_T(t, (A2[:, ps, :], modc[:, 24:32, ps]), h2T)
                    w2v = w_ff2.rearrange("(f p) n -> p f n", p=128)
                    cnt = 0
                    items = []
                    for half in range(2):
                        items += [("w1", half, blk) for blk in range(4)]
                        items += [("w2", half, m) for m in range(8)]
                    kcount = {"w1": 0, "w2": 0}
                    wbufs = {}

                    def issue_load(j):
                        kind, half, i = items[j]
                        kk_ = kcount[kind]
                        kcount[kind] += 1
                        if kind == "w1":
                            wb = w1b[kk_ % 2]
                            c0 = half * 2048 + i * 512
                            dma("pool", wb[:], w_ff1[:, c0:c0 + 512].rearrange("(kc p) n -> p kc n", p=128), [], [wb])
                        else:
                            wb = w2b[kk_ % 2]
                            dma("pool", wb[:], w2v[:, half * 16:(half + 1) * 16, i * 128:(i + 1) * 128], [], [wb])
                        wbufs[j] = wb

                    issue_load(0)
                    for jx, (kind, half, i) in enumerate(items):
                        if jx + 1 < len(items):
                            issue_load(jx + 1)
                        wb = wbufs[jx]
                        if kind == "w1":
                            blk = i
                            for j in range(4):
                                f = blk * 4 + j
                                for g in range(2):
                                    gs = slice(g * 512, (g + 1) * 512)
                                    pb, pbd = kb.bank()
                                    for kc in range(8):
                                        mm(pb, wb[:, kc, j * 128:(j + 1) * 128], h2T[:, kc, gs], kc == 0, kc == 7, [wb, h2T], [pbd])
                                    r_ = rl[cnt % 2]
                                    cnt += 1
                                    act(r_[:], pb, AF.Relu, [pbd], [r_])
                                    tt("dve", aT[:, f, gs], r_[:], r_[:], ALU.mult, [r_], [aT])
                        else:
                            m = i
                            for g in range(2):
                                gs = slice(g * 512, (g + 1) * 512)
                                pb, pbd = kb.bank()
                                for f in range(16):
                                    mm(pb, wb[:, f, :], aT[:, f, gs], f == 0, f == 15, [wb, aT], [pbd])
                                if half == 0:
                                    cp("act", fTs[:, m, gs], pb, [pbd], [fTs])
                                else:
                                    tt("dve", fTs[:, m, gs], fTs[:, m, gs], pb, ALU.add, [pbd, fTs], [fTs])
                    for t in range(8):
                        py = [kb.bank(), kb.bank()]
                        for m in range(8):
                            yy, yd = py[m // 4]
                            tr(yy[:, (m % 4) * 128:(m % 4 + 1) * 128], fTs[:, m, t * 128:(t + 1) * 128], ident_f[:], [fTs, ident_f], [yd])
                        post_norm(t, py[0], py[1], 1)
                        dma("sp", Y[ps][t * 128:(t + 1) * 128, :], xs[:, t, :], xs.D(t), [], chan=kb.outch)
                    kb.barrier()
                    ck(6 + 10 * ps)

    except StopBuild:
        pass
    kb.E["sp"].wait_for(kb.outch, kb.outch.count)
    kb.es.close()
    return kb


_CONST = {}


def _consts():
    if not _CONST:
        cos, sins = _rope_tables()
        tri, negi, negs = _dn_masks()
        masks, _ = _get_na_masks()
        _CONST.update(cos=cos, sin=sins, tri=tri, negi=negi, negs=negs,
                      nam=np.ascontiguousarray(masks.reshape(128, -1)), ident=np.eye(128, dtype=np.float32),
                      perm=_rope_perm())
    return _CONST


def make_in_maps(inp):
    C = _consts()
    f = lambda a: np.ascontiguousarray(np.asarray(a, dtype=np.float32))
    w_in = f(inp["w_in"][0])
    shared = dict(
        w_ada=f(inp["w_ada"][0]), b_ada=f(inp["b_ada"][0]),
        nrm=f(np.stack([inp["norm_pre1"][0], inp["norm_post1"][0], inp["norm_pre2"][0], inp["norm_post2"][0]])),
        w_in=w_in, w_qkp=np.ascontiguousarray(w_in[:, C["perm"]]),
        conv_w=f(inp["conv_w"][0]),
        adt=f(np.stack([np.asarray(inp["a_log"][0]).reshape(16), np.asarray(inp["dt_bias"][0]).reshape(16)])),
        dn_norm=f(inp["dn_norm"][0]),
        tb2=np.ascontiguousarray(_rpb_table(f(inp["na_rpb"][0])).reshape(128, -1)),
        w_ao=f(inp["w_ao"][0]), w_do=f(inp["w_do"][0]), w_out=f(inp["w_out"][0]),
        w_ff1=f(inp["w_ff1"][0]), w_ff2=f(inp["w_ff2"][0]),
        ident=C["ident"], tri=C["tri"], negi=C["negi"], negs=C["negs"], nam=C["nam"], cos=C["cos"], sin=C["sin"],
    )
    xp, xsm, c = f(inp["x_prompt"]), f(inp["x_sample"]), f(inp["c"])
    ck, cv, st = f(inp["cache_na_k"]), f(inp["cache_na_v"]), f(inp["state_delta"])
    c_ctx = f(inp["c_ctx"])
    maps = []
    for i in range(8):
        b = i % 4
        m = dict(shared)
        m.update(xc=np.ascontiguousarray(xp[4 * i:4 * i + 4].reshape(1024, 1024)), xl=xsm[b],
                 cvec=np.ascontiguousarray(np.stack([c_ctx, c[b]])), ck=ck[b, 0], cv=cv[b, 0], st=st[b, 0])
        maps.append(m)
    return maps


_PROG = {}


def kernel(**inputs):
    if "kb" not in _PROG:
        _PROG["kb"] = build_program()
    kb = _PROG["kb"]
    maps = make_in_maps(inputs)
    ncores = int(os.environ.get("KCORES", "8"))
    res = run_bass_kernel_spmd(kb.nc, maps[:ncores], core_ids=list(range(ncores)))
    R = list(res.results)
    while len(R) < 8:
        R.append(R[0])
    y_prompt = np.concatenate([R[i]["yc"].reshape(4, 256, 1024) for i in range(8)], axis=0)
    y_sample = np.stack([R[b]["yl"] for b in range(4)], axis=0)
    nk = np.concatenate([R[i]["newk"] for i in range(8)], axis=0)[:, None]
    nv = np.concatenate([R[i]["newv"] for i in range(8)], axis=0)[:, None]
    ns = np.concatenate([R[i]["news"] for i in range(8)], axis=0)[:, None]
    out = (y_prompt.astype(np.float32), y_sample.astype(np.float32), nk.astype(np.float32), nv.astype(np.float32), ns.astype(np.float32))
    if DEBUG:
        _PROG["dbg"] = R
    return out
```

```python
import contextlib
import os
import numpy as np
import concourse.bass as bass
import concourse.mybir as mybir
from concourse.bass_utils import run_bass_kernel_spmd

F32 = mybir.dt.float32
BF16 = mybir.dt.bfloat16
AF = mybir.ActivationFunctionType
ALU = mybir.AluOpType
AX = mybir.AxisListType

DEBUG = False
STAGE = 99


class StopBuild(Exception):
    pass
NEG = -1.0e30
EPS = 1e-6
IN_OFF = dict(q=0, k=512, v=1024, dn=1536, b=3072, a=3088, z=3104, ga=3616, gd=4640)


class Chan:
    def __init__(self, sem, step):
        self.sem, self.step, self.count = sem, step, 0


class Dep:
    __slots__ = ("w", "r")

    def __init__(self):
        self.w = None
        self.r = {}


class Buf:
    def __init__(self, t, keys=None):
        self.t = t
        self.d = {k: Dep() for k in (keys if keys is not None else [None])}

    def __getitem__(self, idx):
        return self.t[idx]

    def D(self, *keys):
        if not keys:
            return list(self.d.values())
        return [self.d[k] for k in keys]


class Eng:
    def __init__(self, name, obj, chan):
        self.name, self.obj, self.chan, self.known = name, obj, chan, {}

    def wait_for(self, chan, value):
        if value <= 0 or self.known.get(chan, 0) >= value:
            return
        self.obj.wait_ge(chan.sem, value)
        self.known[chan] = value


def _deps(lst):
    out = []
    for x in lst:
        if isinstance(x, Buf):
            out.extend(x.D())
        elif isinstance(x, (list, tuple)):
            out.extend(_deps(x))
        else:
            out.append(x)
    return out


class KB:
    def __init__(self):
        self.nc = nc = bass.Bass("TRN2", target_bir_lowering=False)
        self.es = contextlib.ExitStack()
        self.dbg = []

    def start(self):
        nc, es = self.nc, self.es
        sem = lambda n: es.enter_context(nc.semaphore(n))
        self.E = {}
        for nm, ob in [("pe", nc.tensor), ("act", nc.scalar), ("dve", nc.vector), ("pool", nc.gpsimd), ("sp", nc.sync)]:
            self.E[nm] = Eng(nm, ob, Chan(sem("s_" + nm), 1))
        self.dch = [Chan(sem("d%d" % i), 16) for i in range(int(os.environ.get("KNCH", "16")))]
        self.dch_i = 0
        self.dch_ip = 0
        self.outch = Chan(sem("outc"), 16)
        self.PA = Buf(es.enter_context(nc.psum_tensor("PA", [128, 2048], F32)), keys=[0, 1, 2, 3])
        self.PB = Buf(es.enter_context(nc.psum_tensor("PB", [128, 1024], F32)), keys=[0, 1])
        self.PC = Buf(es.enter_context(nc.psum_tensor("PC", [128, 512], F32)), keys=[0])
        self.PD = Buf(es.enter_context(nc.psum_tensor("PD", [128, 512], F32)), keys=[0])
        self.banks = [(self.PA, 0), (self.PA, 1), (self.PA, 2), (self.PA, 3), (self.PB, 0), (self.PB, 1), (self.PC, 0), (self.PD, 0)]
        self.bank_i = 0
        es.enter_context(nc.Block())

    def bank(self, i=None):
        if i is None:
            i = self.bank_i
            self.bank_i = (self.bank_i + 1) % 8
        b, k = self.banks[i]
        return b[:, k * 512:(k + 1) * 512], b.d[k]

    def sb(self, es, name, shape, dt, keys=None):
        self.uid = getattr(self, "uid", 0) + 1
        return Buf(es.enter_context(self.nc.sbuf_tensor("%s_%d" % (name, self.uid), shape, dt)), keys)

    stopped = False

    def emit(self, en, fn, R=(), W=(), chan=None):
        if self.stopped:
            return None
        eng = self.E[en]
        c = chan if chan is not None else eng.chan
        R, W = _deps(R), _deps(W)
        need = {}
        for d in R:
            if d.w is not None:
                need[d.w[0]] = max(need.get(d.w[0], 0), d.w[1])
        for d in W:
            if d.w is not None:
                need[d.w[0]] = max(need.get(d.w[0], 0), d.w[1])
            for ch, v in d.r.items():
                need[ch] = max(need.get(ch, 0), v)
        for ch, v in need.items():
            if ch is eng.chan and en == "pe":
                continue
            eng.wait_for(ch, v)
        ins = fn()
        c.count += c.step
        ins.then_inc(c.sem, c.step)
        for d in R:
            d.r[c] = c.count
        for d in W:
            d.w = (c, c.count)
            d.r = {}
        return ins

    def barrier(self):
        if self.stopped:
            return
        chans = [e.chan for e in self.E.values()] + self.dch + [self.outch]
        for e in self.E.values():
            for ch in chans:
                if ch is e.chan and e.name not in ("pool", "sp"):
                    continue
                e.wait_for(ch, ch.count)

    def mm(self, out, lhsT, rhs, start, stop, R, W):
        nc = self.nc
        return self.emit("pe", lambda: nc.tensor.matmul(out, lhsT=lhsT, rhs=rhs, start=start, stop=stop), R, W)

    def tr(self, out, in_, ident, R, W):
        nc = self.nc
        return self.emit("pe", lambda: nc.tensor.transpose(out, in_, ident), R, W)

    def act(self, out, in_, func, R, W, scale=None, bias=None, accum=None):
        nc = self.nc
        kw = {}
        if scale is not None:
            kw["scale"] = scale
        if bias is not None:
            kw["bias"] = bias
        if accum is not None:
            kw["accum_out"] = accum
        return self.emit("act", lambda: nc.scalar.activation(out, in_, func, **kw), R, W)

    def _v(self, en):
        return self.nc.vector if en == "dve" else self.nc.gpsimd

    def tt(self, en, out, a, b, op, R, W):
        e = self._v(en)
        return self.emit(en, lambda: e.tensor_tensor(out, a, b, op), R, W)

    def ts(self, en, out, a, s1, s2, op0, op1, R, W):
        e = self._v(en)
        if op1 is None:
            return self.emit(en, lambda: e.tensor_scalar(out, a, s1, None, op0), R, W)
        return self.emit(en, lambda: e.tensor_scalar(out, a, s1, s2, op0, op1), R, W)

    def stt(self, out, in0, scalar, in1, op0, op1, R, W):
        nc = self.nc
        return self.emit("dve", lambda: nc.vector.scalar_tensor_tensor(out, in0, scalar, in1, op0, op1), R, W)

    def cp(self, en, out, in_, R, W):
        nc = self.nc
        if en == "act":
            if os.environ.get("KACTCP", "1") == "1":
                return self.emit("act", lambda: nc.scalar.activation(out, in_, AF.Identity), R, W)
            return self.emit("act", lambda: nc.scalar.copy(out, in_), R, W)
        e = self._v(en)
        return self.emit(en, lambda: e.tensor_copy(out, in_), R, W)

    def memset(self, en, ap, val, W):
        e = self._v(en)
        return self.emit(en, lambda: e.memset(ap, val), (), W)

    def recip(self, out, in_, R, W):
        nc = self.nc
        return self.emit("dve", lambda: nc.vector.reciprocal(out, in_), R, W)

    def reduce_sum(self, out, in_, R, W):
        nc = self.nc
        return self.emit("dve", lambda: nc.vector.tensor_reduce(out, in_, AX.X, ALU.add), R, W)

    def dma(self, en, out, in_, R, W, chan=None, slow=False):
        eng = self.E[en]
        if chan is None:
            half = len(self.dch) // 2
            if en == "pool":
                chan = self.dch[half + self.dch_ip]
                self.dch_ip = (self.dch_ip + 1) % (len(self.dch) - half)
            else:
                chan = self.dch[self.dch_i]
                self.dch_i = (self.dch_i + 1) % half
            if not self.stopped:
                eng.wait_for(chan, chan.count)
        o = eng.obj
        if slow:
            r = self.emit(en, lambda: o.dma_start(out=out, in_=in_, allow_slow_non_contiguous=True), R, W, chan=chan)
        else:
            r = self.emit(en, lambda: o.dma_start(out=out, in_=in_), R, W, chan=chan)
        if os.environ.get("KSYNC", "0") == "1" and not self.stopped:
            eng.wait_for(chan, chan.count)
        return r

    def dump(self, name, ap, R):
        if self.stopped:
            return
        if not DEBUG and name not in os.environ.get("KDUMP", "").split(","):
            return
        shape = list(ap.shape)
        dt = ap.dtype
        o = self.nc.dram_tensor("dbg_" + name, shape, dt, kind="ExternalOutput").ap()
        self.dma("sp", o, ap, R, [])
        self.dbg.append("dbg_" + name)


def _rope_tables():
    t = np.arange(1024)
    row, col = t // 64, t % 64
    inv = 10000.0 ** (-np.arange(16, dtype=np.float32) / 16.0)
    cos = np.zeros((128, 1024), np.float32)
    sins = np.zeros((128, 1024), np.float32)
    for p in range(128):
        d = p % 64
        pos = row if d < 32 else col
        ang = pos.astype(np.float32) * inv[d % 16]
        cos[p] = np.cos(ang)
        s = np.sin(ang)
        sins[p] = -s if (d % 32) < 16 else s
    return cos, sins


def _rope_perm():
    perm = np.zeros(1024, np.int64)
    for c in range(1024):
        blk, hh, d = c // 512, (c % 512) // 64, c % 64
        pd = d + 16 if (d % 32) < 16 else d - 16
        perm[c] = blk * 512 + hh * 64 + pd
    return perm


def _rs(r):
    return int(np.clip(r - 4, 0, 8))


def _cs(c):
    return int(np.clip(c - 8, 0, 48))


def na_tiles(a):
    lo = _rs(2 * a) // 2
    hi = (_rs(2 * a + 1) + 7) // 2
    return list(range(lo, hi + 1))


def _na_masks():
    tiles = []
    index = {}
    for a in range(8):
        for kt in na_tiles(a):
            m = np.full((128, 128), NEG, np.float32)
            for p in range(128):
                kr, kc = 2 * kt + p // 64, p % 64
                for q in range(128):
                    qr, qc = 2 * a + q // 64, q % 64
                    if _rs(qr) <= kr < _rs(qr) + 8 and _cs(qc) <= kc < _cs(qc) + 16:
                        m[p, q] = 0.0
            index[(a, kt)] = len(tiles)
            tiles.append(m)
    return np.stack(tiles, axis=1), index


_NA_MASKS, NA_MASK_INDEX = None, None


def _get_na_masks():
    global _NA_MASKS, NA_MASK_INDEX
    if _NA_MASKS is None:
        tiles, index = [], {}
        p = np.arange(128)
        q = np.arange(128)
        for a in range(8):
            for kt in na_tiles(a):
                kr = (2 * kt + p // 64)[:, None]
                kc = (p % 64)[:, None]
                qr = (2 * a + q // 64)[None, :]
                qc = (q % 64)[None, :]
                rs = np.clip(qr - 4, 0, 8)
                cs = np.clip(qc - 8, 0, 48)
                valid = (kr >= rs) & (kr < rs + 8) & (kc >= cs) & (kc < cs + 16)
                index[(a, kt)] = len(tiles)
                tiles.append(np.where(valid, 0.0, NEG).astype(np.float32))
        _NA_MASKS, NA_MASK_INDEX = np.ascontiguousarray(np.stack(tiles, axis=1)), index
    return _NA_MASKS, NA_MASK_INDEX


def _rpb_table(rpb):
    out = np.zeros((128, 8, 16, 64), np.float32)
    kc = np.arange(64)[:, None]
    qc = np.arange(64)[None, :]
    dc = np.clip(kc - qc + 15, 0, 30)
    for m in range(16):
        if 0 <= 14 - m <= 14:
            out[0:64, :, m, :] = np.transpose(rpb[:, 14 - m][:, dc], (1, 0, 2))
        if 0 <= 15 - m <= 14:
            out[64:128, :, m, :] = np.transpose(rpb[:, 15 - m][:, dc], (1, 0, 2))
    return out


def _dn_masks():
    tri = np.zeros((2, 128, 128), np.float32)
    negi = np.full((2, 128, 128), NEG, np.float32)
    negs = np.full((2, 128, 128), NEG, np.float32)
    j = np.arange(64)[:, None]
    i = np.arange(64)[None, :]
    for v in range(2):
        for half in range(2):
            fwd = (half == v)
            sl = slice(half * 64, half * 64 + 64)
            if fwd:
                tri[v, sl, sl] = (j <= i)
                negi[v, sl, sl] = np.where(i >= j, 0.0, NEG)
                negs[v, sl, sl] = np.where(i > j, 0.0, NEG)
            else:
                tri[v, sl, sl] = (j >= i)
                negi[v, sl, sl] = np.where(i <= j, 0.0, NEG)
                negs[v, sl, sl] = np.where(i < j, 0.0, NEG)
    return tri, negi, negs


def build_program():
    kb = KB()
    nc = kb.nc

    def din(name, shape, dt=F32):
        return nc.dram_tensor(name, list(shape), dt, kind="ExternalInput").ap()

    def dout(name, shape, dt=F32):
        return nc.dram_tensor(name, list(shape), dt, kind="ExternalOutput").ap()

    X = [din("xc", [1024, 1024]), din("xl", [1024, 1024])]
    cvec = din("cvec", [2, 1024])
    ckd = din("ck", [8, 256, 64])
    cvd = din("cv", [8, 256, 64])
    std = din("st", [2, 8, 64, 64])
    w_ada = din("w_ada", [1024, 6144])
    b_ada = din("b_ada", [6144])
    nrm = din("nrm", [4, 1024])
    w_in = din("w_in", [1024, 5664])
    w_qkp = din("w_qkp", [1024, 1024])
    conv_w = din("conv_w", [5, 1536])
    adt = din("adt", [2, 16])
    dn_norm = din("dn_norm", [64])
    tb2d = din("tb2", [128, 8 * 16 * 64])
    w_ao = din("w_ao", [512, 1024])
    w_do = din("w_do", [512, 1024])
    w_out = din("w_out", [1024, 1024])
    w_ff1 = din("w_ff1", [1024, 4096])
    w_ff2 = din("w_ff2", [4096, 1024])
    identd = din("ident", [128, 128])
    trid = din("tri", [2, 128, 128])
    negid = din("negi", [2, 128, 128])
    negsd = din("negs", [2, 128, 128])
    masks_np, mindex = _get_na_masks()
    NMT = masks_np.shape[1]
    nmd = din("nam", [128, NMT * 128])
    cosd = din("cos", [128, 1024])
    sind = din("sin", [128, 1024])

    Y = [dout("yc", [1024, 1024]), dout("yl", [1024, 1024])]
    newk = dout("newk", [4, 8, 256, 64])
    newv = dout("newv", [4, 8, 256, 64])
    news = dout("news", [4, 2, 8, 64, 64])
    if os.environ.get("KPAD", "0") == "1":
        padt = [dout("padt%d" % i, [1024, 1024]) for i in range(4)]

    kb.start()
    es0 = kb.es

    for _e in ("pe", "act", "dve", "pool"):
        for _i in range(int(os.environ.get("KNOP_" + _e, os.environ.get("KNOP", "0")))):
            kb.E[_e].obj.nop()

    def ck(stage):
        if STAGE == stage and not kb.stopped:
            kb.barrier()
            kb.stopped = True
    mm, tr, act, tt, ts, stt, cp, dma = kb.mm, kb.tr, kb.act, kb.tt, kb.ts, kb.stt, kb.cp, kb.dma

    try:
        ident_f = kb.sb(es0, "ident_f", [128, 128], F32)
        ident_b = kb.sb(es0, "ident_b", [128, 128], BF16)
        ones_f = kb.sb(es0, "ones_f", [128, 128], F32)
        blk1 = kb.sb(es0, "blk1", [128, 128], BF16)
        modc = kb.sb(es0, "modc", [128, 48, 2], F32)
        nrmc = kb.sb(es0, "nrmc", [128, 4, 8], F32)
        A1 = kb.sb(es0, "A1", [128, 2, 8], F32)
        A2 = kb.sb(es0, "A2", [128, 2, 8], F32)
        Gc = kb.sb(es0, "Gc", [128, 2, 2, 8], F32)
        Grow = kb.sb(es0, "Grow", [128, 2, 2, 1024], F32, keys=[(p, w) for p in range(2) for w in range(2)])
        tri_s = kb.sb(es0, "tri_s", [128, 2, 128], F32)
        negi_s = kb.sb(es0, "negi_s", [128, 2, 128], F32)
        negs_s = kb.sb(es0, "negs_s", [128, 2, 128], F32)
        tri_b = kb.sb(es0, "tri_b", [128, 2, 128], BF16)
        ones_b = kb.sb(es0, "ones_b", [128, 128], BF16)
        dnw = kb.sb(es0, "dnw", [128, 64], F32)
        adt_s = kb.sb(es0, "adt_s", [128, 2, 16], F32)
        nea = kb.sb(es0, "nea", [128, 16], F32)
        convc = kb.sb(es0, "convc", [128, 5, 12], F32)

        dma("sp", ident_f[:], identd, [], [ident_f])
        dma("pool", ident_b[:], identd, [], [ident_b])
        kb.memset("dve", ones_f[:], 1.0, [ones_f])
        kb.memset("pool", blk1[:], 0.0, [blk1])
        kb.memset("pool", blk1[0:64, 0:64], 1.0, [blk1])
        kb.memset("pool", blk1[64:128, 64:128], 1.0, [blk1])
        dma("sp", nrmc[:], nrm.rearrange("w (c p) -> p w c", p=128), [], [nrmc], slow=True)
        dma("sp", tri_s[:], trid.rearrange("v p n -> p v n"), [], [tri_s])
        dma("sp", negi_s[:], negid.rearrange("v p n -> p v n"), [], [negi_s])
        dma("sp", negs_s[:], negsd.rearrange("v p n -> p v n"), [], [negs_s])
        cp("dve", tri_b[:], tri_s[:], [tri_s], [tri_b])
        kb.memset("dve", ones_b[:], 1.0, [ones_b])
        dma("sp", dnw[:], dn_norm.partition_broadcast(128), [], [dnw])
        dma("sp", adt_s[:], adt.partition_broadcast(128), [], [adt_s])
        for j in range(5):
            dma("sp", convc[:, j, :], conv_w[j].rearrange("(c p) -> p c", p=128), [], [convc], slow=True)
        act(nea[:], adt_s[:, 0, :], AF.Exp, [adt_s], [nea])
        ts("dve", nea[:], nea[:], -1.0, None, ALU.mult, None, [nea], [nea])
        with contextlib.ExitStack() as es:
            c2 = kb.sb(es, "c2", [128, 2, 8], F32)
            sc2 = kb.sb(es, "sc2", [128, 2, 8], F32)
            bad = kb.sb(es, "bad", [128, 48], F32)
            wa = [kb.sb(es, "wa%d" % i, [128, 8, 768], F32) for i in range(2)]
            for r_ in range(2):
                dma("sp", c2[:, r_, :], cvec[r_].rearrange("(c p) -> p c", p=128), [], [c2], slow=True)
            dma("sp", bad[:], b_ada.rearrange("(m p) -> p m", p=128), [], [bad], slow=True)
            act(sc2[:], c2[:], AF.Silu, [c2], [sc2])
            pmod, pmd = kb.bank(7)
            for blk in range(8):
                wb = wa[blk % 2]
                dma("sp", wb[:], w_ada[:, blk * 768:(blk + 1) * 768].rearrange("(kc p) n -> p kc n", p=128), [], [wb])
                for mi in range(6):
                    m = blk * 6 + mi
                    for kc in range(8):
                        mm(pmod[:, m * 2:m * 2 + 2], wb[:, kc, mi * 128:(mi + 1) * 128], sc2[:, :, kc], kc == 0, kc == 7,
                           [wb, sc2], [pmd])
            tt("dve", modc[:], pmod[:, 0:96].rearrange("p (m r) -> p m r", r=2), bad[:].unsqueeze(2).to_broadcast([128, 48, 2]),
               ALU.add, [pmd, bad], [modc])
            for p in range(2):
                stt(A1[:, p, :], modc[:, 8:16, p], 1.0, nrmc[:, 0, :], ALU.add, ALU.mult, [modc, nrmc], [A1])
                stt(A2[:, p, :], modc[:, 32:40, p], 1.0, nrmc[:, 2, :], ALU.add, ALU.mult, [modc, nrmc], [A2])
                tt("dve", Gc[:, p, 0, :], modc[:, 16:24, p], nrmc[:, 1, :], ALU.mult, [modc, nrmc], [Gc])
                tt("dve", Gc[:, p, 1, :], modc[:, 40:48, p], nrmc[:, 3, :], ALU.mult, [modc, nrmc], [Gc])
            dg = kb.sb(es, "dg", [128, 128], F32)
            for p in range(2):
                for w in range(2):
                    for c in range(8):
                        ts("dve", dg[:], ident_f[:], Gc[:, p, w, c:c + 1], None, ALU.mult, None, [ident_f, Gc], [dg])
                        pb, pbd = kb.bank()
                        mm(pb[:, 0:128], ones_f[:], dg[:], True, True, [ones_f, dg], [pbd])
                        cp("act", Grow[:, p, w, c * 128:(c + 1) * 128], pb[:, 0:128], [pbd], Grow.D((p, w)))
            kb.barrier()
            ck(0)

        for ps in range(2):
            lat = (ps == 1)
            if os.environ.get("KSKIP", "") == str(ps):
                continue
            T = 1024 if lat else 256
            nseq = 1 if lat else 4
            nch = T // 64
            with contextlib.ExitStack() as esP:
                xs = kb.sb(esP, "xs", [128, 8, 1024], F32, keys=list(range(8)))
                hT = kb.sb(esP, "hT", [128, 8, 1024], BF16, keys=list(range(8)))
                oT = kb.sb(esP, "oT", [128, 2, 4, 1024], BF16, keys=[(w, t) for w in range(2) for t in range(8)])
                rst = kb.sb(esP, "rst", [128, 8], F32, keys=list(range(8)))
                ssq = kb.sb(esP, "ssq", [128, 8], F32, keys=list(range(8)))
                xn = [kb.sb(esP, "xn%d" % i, [128, 1024], BF16) for i in range(2)]
                junk = kb.sb(esP, "junk", [128, 1024], BF16)

                def norm_to_T(t, Acol, dstT):
                    act(junk[:], xs[:, t, :], AF.Square, xs.D(t), [junk] + ssq.D(t), accum=ssq[:, t:t + 1])
                    act(rst[:, t:t + 1], ssq[:, t:t + 1], AF.Sqrt, ssq.D(t), rst.D(t), scale=1.0 / 1024.0, bias=EPS)
                    kb.recip(rst[:, t:t + 1], rst[:, t:t + 1], rst.D(t), rst.D(t))
                    xb = xn[t % 2]
                    ts("dve", xb[:], xs[:, t, :], rst[:, t:t + 1], None, ALU.mult, None, xs.D(t) + rst.D(t), [xb])
                    pb, pbd = kb.bank()
                    pbb = pb.bitcast(BF16)
                    for c in range(8):
                        tr(pbb[:, c * 128:(c + 1) * 128], xb[:, c * 128:(c + 1) * 128], ident_b[:], [xb, ident_b], [pbd])
                    A, Bc = Acol
                    for c in range(8):
                        if c % 2 == 0:
                            act(dstT[:, c, t * 128:(t + 1) * 128], pbb[:, c * 128:(c + 1) * 128], AF.Identity, [pbd, A1, A2, modc],
                                dstT.D(t), scale=A[:, c:c + 1], bias=Bc[:, c:c + 1])
                        else:
                            ts("dve", dstT[:, c, t * 128:(t + 1) * 128], pbb[:, c * 128:(c + 1) * 128], A[:, c:c + 1], Bc[:, c:c + 1],
                               ALU.mult, ALU.add, [pbd, A1, A2, modc], dstT.D(t))

                for t in range(8):
                    dma("sp", xs[:, t, :], X[ps][t * 128:(t + 1) * 128, :], [], xs.D(t))
                for t in range(8):
                    norm_to_T(t, (A1[:, ps, :], modc[:, 0:8, ps]), hT)
                kb.dump("hT%d" % ps, hT[:], [hT])
                ck(1 + 10 * ps)

                with contextlib.ExitStack() as esA:
                    dqT = kb.sb(esA, "dqT", [128, 4, 1024], BF16)
                    dkT = kb.sb(esA, "dkT", [128, 4, 1024], BF16)
                    k_tok = kb.sb(esA, "k_tok", [128, 8, 512], BF16)
                    v_tok = kb.sb(esA, "v_tok", [128, 8, 512], BF16)
                    zs = kb.sb(esA, "zs", [128, 8, 512], BF16)
                    gb = kb.sb(esA, "gb", [128, 8, 32], F32)
                    with contextlib.ExitStack() as esA1:
                        qT = kb.sb(esA1, "qT", [128, 4, 1024], BF16)
                        kT = kb.sb(esA1, "kT", [128, 4, 1024], BF16)
                        v_na = kb.sb(esA1, "v_na", [128, 8, 8, 66], BF16)
                        with contextlib.ExitStack() as es:
                            wbuf = [kb.sb(es, "wbuf%d" % i, [128, 8, 512], BF16) for i in range(2)]
                            wi = [0]

                            plan = []
                            for blk_ in range(2):
                                plan.append((w_in[:, blk_ * 512:(blk_ + 1) * 512], 512, False))
                                if lat:
                                    plan.append((w_qkp[:, blk_ * 512:(blk_ + 1) * 512], 512, True))
                            plan.append((w_in[:, 1024:1536], 512, False))
                            if not lat:
                                plan.append((w_in[:, 512:1024], 512, False))
                            plan.append((w_in[:, 3104:3616], 512, False))
                            plan.append((w_in[:, 3072:3104], 32, False))
                            for blk_ in range(3):
                                plan.append((w_in[:, 1536 + blk_ * 512:1536 + (blk_ + 1) * 512], 512, False))
                            issued = {}

                            def _issue(i):
                                if i in issued or i >= len(plan):
                                    return
                                src_, nc_, _ = plan[i]
                                b = wbuf[i % 2]
                                dma("pool", b[:, :, 0:nc_], src_.rearrange("(kc p) n -> p kc n", p=128), [], [b])
                                issued[i] = b

                            def load_w(src, ncols):
                                i = wi[0]
                                wi[0] += 1
                                assert plan[i][1] == ncols
                                _issue(i)
                                if not plan[i][2]:
                                    _issue(i + 1)
                                return issued[i]

                            def fm_group(wb, j, g, extra_R=()):
                                pb, pbd = kb.bank()
                                for kc in range(8):
                                    mm(pb, wb[:, kc, j * 128:(j + 1) * 128], hT[:, kc, g * 512:(g + 1) * 512], kc == 0, kc == 7,
                                       [wb] + hT.D(*range(4 * g, 4 * g + 4)), [pbd])
                                return pb, pbd

                            def tm_group(wb, t, ncols=512):
                                pb, pbd = kb.bank()
                                for kc in range(8):
                                    mm(pb[:, 0:ncols], hT[:, kc, t * 128:(t + 1) * 128], wb[:, kc, 0:ncols], kc == 0, kc == 7,
                                       [wb] + hT.D(t), [pbd])
                                return pb, pbd

                            if lat:
                                cos_s = kb.sb(es, "cos_s", [128, 1024], F32)
                                sin_s = kb.sb(es, "sin_s", [128, 1024], F32)
                                dma("sp", cos_s[:], cosd, [], [cos_s])
                                dma("sp", sin_s[:], sind, [], [sin_s])
                                rt = [kb.sb(es, "rt%d" % i, [128, 512], F32) for i in range(2)]
                            for blk in range(2):
                                wb = load_w(w_in[:, blk * 512:(blk + 1) * 512], 512)
                                dst = qT if blk == 0 else kT
                                if lat:
                                    wp = load_w(w_qkp[:, blk * 512:(blk + 1) * 512], 512)
                                for j in range(4):
                                    for g in range(2):
                                        pb, pbd = fm_group(wb, j, g)
                                        if not lat:
                                            cp("act", dst[:, j, g * 512:(g + 1) * 512], pb, [pbd], [dst])
                                        else:
                                            pb2, pbd2 = fm_group(wp, j, g)
                                            tt("dve", rt[0][:], pb, cos_s[:, g * 512:(g + 1) * 512], ALU.mult, [pbd, cos_s], [rt[0]])
                                            tt("dve", rt[1][:], pb2, sin_s[:, g * 512:(g + 1) * 512], ALU.mult, [pbd2, sin_s], [rt[1]])
                                            tt("pool", dst[:, j, g * 512:(g + 1) * 512], rt[0][:], rt[1][:], ALU.add, [rt[0], rt[1]], [dst])
                            ck(21 + 100 * ps)
                            kb.memset("pool", v_na[:], 1.0, [v_na])
                            stg = [kb.sb(es, "stg%d" % i, [128, 512], F32) for i in range(2)]
                            wv = load_w(w_in[:, 1024:1536], 512)
                            for t in range(8):
                                pb, pbd = tm_group(wv, t)
                                cp("dve", v_na[:, t, :, 0:64], pb.rearrange("p (h d) -> p h d", d=64), [pbd], [v_na])
                                if not lat:
                                    sg = stg[t % 2]
                                    cp("dve", sg[:], pb, [pbd], [sg])
                                    sq, half = t // 2, t % 2
                                    dma("sp", newv[sq, :, half * 128:(half + 1) * 128, :].rearrange("h s d -> s h d"),
                                        sg[:].rearrange("p (h d) -> p h d", d=64), [sg], [])
                            if not lat:
                                wk = load_w(w_in[:, 512:1024], 512)
                                for t in range(8):
                                    pb, pbd = tm_group(wk, t)
                                    sg = stg[t % 2]
                                    cp("dve", sg[:], pb, [pbd], [sg])
                                    sq, half = t // 2, t % 2
                                    dma("sp", newk[sq, :, half * 128:(half + 1) * 128, :].rearrange("h s d -> s h d"),
                                        sg[:].rearrange("p (h d) -> p h d", d=64), [sg], [])
                            ck(22 + 100 * ps)
                            wz = load_w(w_in[:, 3104:3616], 512)
                            for t in range(8):
                                pb, pbd = tm_group(wz, t)
                                act(zs[:, t, :], pb, AF.Silu, [pbd], [zs])
                            ck(23 + 100 * ps)
                            wba = load_w(w_in[:, 3072:3104], 32)
                            glog = kb.sb(es, "glog", [128, 8, 16], F32)
                            for t in range(8):
                                pb, pbd = tm_group(wba, t, 32)
                                act(gb[:, t, 0:16], pb[:, 0:16], AF.Sigmoid, [pbd], [gb])
                                tt("dve", glog[:, t, :], pb[:, 16:32], adt_s[:, 1, :], ALU.add, [pbd, adt_s], [glog])
                            act(glog[:], glog[:], AF.Exp, [glog], [glog])
                            act(glog[:], glog[:], AF.Ln, [glog], [glog], bias=1.0)
                            tt("dve", gb[:, :, 16:32], glog[:], nea[:].unsqueeze(1).to_broadcast([128, 8, 16]), ALU.mult,
                               [glog, nea], [gb])
                            ck(24 + 100 * ps)
                            cpre = kb.sb(es, "cpre", [128, nseq, T + 4], BF16)
                            kb.memset("pool", cpre[:], 0.0, [cpre])
                            sil = [kb.sb(es, "sil%d" % i, [128, 512], F32) for i in range(2)]
                            sqb = [kb.sb(es, "sqb%d" % i, [128, 512], BF16) for i in range(2)]
                            rno = [kb.sb(es, "rno%d" % i, [128, 512], F32) for i in range(2)]
                            vfm = kb.sb(es, "vfm", [128, 1024], BF16)
                            kfm_i = [0]
                            convd = [kb.sb(es, "convd%d" % i, [128, 5, 128], BF16) for i in range(2)]
                            for blk in range(3):
                                wb = load_w(w_in[:, 1536 + blk * 512:1536 + (blk + 1) * 512], 512)
                                for j in range(4):
                                    c = blk * 4 + j
                                    cdg = convd[c % 2]
                                    for jj in range(5):
                                        ts("dve" if jj % 2 else "pool", cdg[:, jj, :], ident_f[:], convc[:, jj, c:c + 1], None, ALU.mult, None,
                                           [ident_f, convc], [cdg])
                                    for g in range(2):
                                        pb, pbd = fm_group(wb, j, g)
                                        if lat:
                                            cp("act", cpre[:, 0, 2 + g * 512:2 + (g + 1) * 512], pb, [pbd], [cpre])
                                        else:
                                            cp("act", cpre[:, 2 * g:2 * g + 2, 2:2 + 256], pb.rearrange("p (s t) -> p s t", s=2), [pbd], [cpre])
                                    for g in range(2):
                                        pc, pcd = kb.bank()
                                        for jj in range(5):
                                            if lat:
                                                rhs = cpre[:, 0, g * 512 + jj:g * 512 + jj + 512]
                                            else:
                                                rhs = cpre[:, 2 * g:2 * g + 2, jj:jj + 256]
                                            mm(pc, cdg[:, jj, :], rhs, jj == 0, jj == 4, [cdg, cpre], [pcd])
                                        i2 = kfm_i[0] % 2
                                        kfm_i[0] += 1
                                        gs = slice(g * 512, (g + 1) * 512)
                                        if blk == 2:
                                            act(vfm[:, gs], pc, AF.Silu, [pcd], [vfm])
                                        else:
                                            s_ = sil[i2]
                                            act(s_[:], pc, AF.Silu, [pcd], [s_])
                                            tt("pool", sqb[i2][:], s_[:], s_[:], ALU.mult, [s_], [sqb[i2]])
                                            pn, pnd = kb.bank()
                                            mm(pn, blk1[:], sqb[i2][:], True, True, [blk1, sqb[i2]], [pnd])
                                            act(rno[i2][:], pn, AF.Sqrt, [pnd], [rno[i2]], bias=EPS)
                                            kb.recip(rno[i2][:], rno[i2][:], [rno[i2]], [rno[i2]])
                                            dst = dqT if blk == 0 else dkT
                                            if blk == 0:
                                                stt(dst[:, j, gs], s_[:], 0.125, rno[i2][:], ALU.mult, ALU.mult, [s_, rno[i2]], [dst])
                                            else:
                                                tt("dve", dst[:, j, gs], s_[:], rno[i2][:], ALU.mult, [s_, rno[i2]], [dst])
                                    if blk >= 1:
                                        src = dkT[:, j, :] if blk == 1 else vfm[:]
                                        srcd = dkT if blk == 1 else vfm
                                        dstb = k_tok if blk == 1 else v_tok
                                        for t in range(8):
                                            pb, pbd = kb.bank()
                                            pbb = pb.bitcast(BF16)
                                            tr(pbb[:, 0:128], src[:, t * 128:(t + 1) * 128], ident_b[:], [srcd, ident_b], [pbd])
                                            cp("act" if t % 2 else "dve", dstb[:, t, j * 128:(j + 1) * 128], pbb[:, 0:128], [pbd], [dstb])
                            kb.dump("qT%d" % ps, qT[:], [qT])
                            kb.dump("kT%d" % ps, kT[:], [kT])
                            kb.dump("dqT%d" % ps, dqT[:], [dqT])
                            kb.dump("dkT%d" % ps, dkT[:], [dkT])
                            kb.dump("vtok%d" % ps, v_tok[:], [v_tok])
                            kb.dump("gb%d" % ps, gb[:], [gb])
                            kb.dump("zs%d" % ps, zs[:], [zs])
                            kb.barrier()
                            ck(2 + 10 * ps)

                        with contextlib.ExitStack() as es:
                            pt = [kb.sb(es, "pt%d" % i, [128, 7, 128], BF16) for i in range(2)]
                            ona = kb.sb(es, "ona", [128, 512], BF16)
                            rsum = kb.sb(es, "rsum", [128, 8], F32)
                            if lat:
                                tb2 = kb.sb(es, "tb2", [128, 8, 16, 64], BF16)
                                nam = kb.sb(es, "nam", [128, NMT, 128], BF16)
                                ckt = kb.sb(es, "ckt", [128, 2, 512], BF16)
                                ckT = kb.sb(es, "ckT", [128, 2, 4, 256], BF16)
                                kTz = kb.sb(es, "kTz", [128, 4, 1024], BF16)
                                kb.memset("pool", kTz[0:64, :, :], 0.0, [kTz])
                                kb.memset("pool", ckT[0:64, 1, :, :], 0.0, [ckT])
                                cp("pool", kTz[64:128, :, :], kT[64:128, :, :], [kT], [kTz])
                                cva = kb.sb(es, "cva", [128, 2, 8, 66], BF16)
                                dma("pool", tb2[:].rearrange("p h m q -> p (h m q)"), tb2d, [], [tb2])
                                dma("pool", nam[:].rearrange("p t q -> p (t q)"), nmd, [], [nam])
                                kb.memset("dve", cva[:], 1.0, [cva])
                                for kt2 in range(2):
                                    dma("pool", ckt[:, kt2, :].rearrange("p (h d) -> p h d", d=64),
                                        ckd[:, kt2 * 128:(kt2 + 1) * 128, :].rearrange("h s d -> s h d"), [], [ckt])
                                    dma("pool", cva[:, kt2, :, 0:64],
                                        cvd[:, kt2 * 128:(kt2 + 1) * 128, :].rearrange("h s d -> s h d"), [], [cva])
                                ts("dve", tb2[:], tb2[:], 8.0, None, ALU.mult, None, [tb2], [tb2])
                                for kt2 in range(2):
                                    for pr in range(4):
                                        pb, pbd = kb.bank(6 + (pr % 2))
                                        pbb = pb.bitcast(BF16)
                                        tr(pbb[:, 0:128], ckt[:, kt2, pr * 128:(pr + 1) * 128], ident_b[:], [ckt, ident_b], [pbd])
                                        cp("dve", ckT[:, 0, pr, kt2 * 128:(kt2 + 1) * 128], pbb[:, 0:128], [pbd], [ckT])
                                        cp("dve", ckT[64:128, 1, pr, kt2 * 128:(kt2 + 1) * 128], pbb[64:128, 0:128], [pbd], [ckT])
                            n_qt = 8
                            for qi in range(n_qt):
                                if lat:
                                    ltiles = na_tiles(qi)
                                    ktl = [("l", kt) for kt in ltiles] + [("c", 0), ("c", 1)]
                                else:
                                    sq = qi // 2
                                    ktl = [("l", 2 * sq), ("l", 2 * sq + 1)]
                                nk = len(ktl)
                                pvb = [kb.bank(4), kb.bank(5)]
                                for h in range(8):
                                    pr, hb = h // 2, (h % 2) * 64
                                    par = (qi * 8 + h) % 2
                                    sA, sAd = kb.bank(2 * par)
                                    sB, sBd = kb.bank(2 * par + 1)
                                    P_ = pt[par]
                                    zp = lat and (h % 2 == 1)
                                    qap = qT[:, pr, qi * 128:(qi + 1) * 128] if zp else qT[hb:hb + 64, pr, qi * 128:(qi + 1) * 128]
                                    for idx, (kind, kt) in enumerate(ktl):
                                        sap, sd = (sA, sAd) if idx < 4 else (sB, sBd)
                                        o_ = sap[:, (idx % 4) * 128:(idx % 4 + 1) * 128]
                                        if kind == "l":
                                            kap = kTz[:, pr, kt * 128:(kt + 1) * 128] if zp else kT[hb:hb + 64, pr, kt * 128:(kt + 1) * 128]
                                            if lat:
                                                m0 = 7 - (2 * kt - 2 * qi)
                                                mm(o_, kap, qap, True, False, [kTz, kT, qT], [sd])
                                                mm(o_, ident_b[:], tb2[:, h, m0:m0 + 2, :], False, False, [ident_b, tb2], [sd])
                                                mm(o_, ident_b[:], nam[:, mindex[(qi, kt)], :], False, True, [ident_b, nam], [sd])
                                            else:
                                                mm(o_, kap, qap, True, True, [kT, qT], [sd])
                                        else:
                                            mm(o_, ckT[:, 1, pr, kt * 128:(kt + 1) * 128] if zp else ckT[0:64, 0, pr, kt * 128:(kt + 1) * 128], qap, True, True, [ckT, qT], [sd])
                                    n1 = min(nk, 4)
                                    act(P_[:, 0:n1, :], sA[:, 0:n1 * 128].rearrange("p (k q) -> p k q", q=128), AF.Exp, [sAd], [P_], scale=0.125)
                                    if nk > 4:
                                        act(P_[:, 4:nk, :], sB[:, 0:(nk - 4) * 128].rearrange("p (k q) -> p k q", q=128), AF.Exp, [sBd], [P_], scale=0.125)
                                    pv, pvd = pvb[h // 4]
                                    o_ = pv[:, (h % 4) * 65:(h % 4) * 65 + 65]
                                    for idx, (kind, kt) in enumerate(ktl):
                                        if kind == "l":
                                            vap, vd = v_na[:, kt, h, 0:65], v_na
                                        else:
                                            vap, vd = cva[:, kt, h, 0:65], cva
                                        mm(o_, P_[:, idx, :], vap, idx == 0, idx == nk - 1, [P_, vd], [pvd])
                                for hg in range(2):
                                    pv, pvd = pvb[hg]
                                    pv3 = pv[:, 0:260].rearrange("p (h e) -> p h e", e=65)
                                    kb.recip(rsum[:, hg * 4:hg * 4 + 4], pv3[:, :, 64], [pvd], [rsum])
                                    tt("dve", ona[:, hg * 256:(hg + 1) * 256].rearrange("p (h d) -> p h d", d=64), pv3[:, :, 0:64],
                                       rsum[:, hg * 4:hg * 4 + 4].unsqueeze(2).to_broadcast([128, 4, 64]), ALU.mult, [pvd, rsum], [ona])
                                for c in range(4):
                                    pb, pbd = kb.bank(6 + (c % 2))
                                    pbb = pb.bitcast(BF16)
                                    tr(pbb[:, 0:128], ona[:, c * 128:(c + 1) * 128], ident_b[:], [ona, ident_b], [pbd])
                                    cp("act", oT[:, 0, c, qi * 128:(qi + 1) * 128], pbb[:, 0:128], [pbd], oT.D((0, qi)))
                            kb.dump("onaT%d" % ps, oT[:, 0, :, :], [oT])
                            kb.barrier()
                            ck(3 + 10 * ps)
                    with contextlib.ExitStack() as es:
                        o_dn = kb.sb(es, "o_dn", [128, 8, 512], F32, keys=list(range(8)))
                        g_st = kb.sb(es, "g_st", [128, 8], F32)
                        b_st = kb.sb(es, "b_st", [128, 8], F32)
                        gc_st = kb.sb(es, "gc_st", [128, 8], F32)
                        ghl = kb.sb(es, "ghl", [128, 2, 8], BF16)
                        egc = kb.sb(es, "egc", [128, 8], F32)
                        ekd = kb.sb(es, "ekd", [128, 8], F32)
                        egl = kb.sb(es, "egl", [128, 2, 8], F32)
                        DG = kb.sb(es, "DG", [128, 8, 128], F32)
                        DF = kb.sb(es, "DF", [128, 8, 128], F32)
                        DTi = kb.sb(es, "DTi", [128, 8, 128], BF16)
                        DTs = kb.sb(es, "DTs", [128, 8, 128], BF16)
                        Pk = [kb.sb(es, "Pk%d" % i, [128, 8, 128], F32, keys=[0, 1, 2, 3]) for i in range(2)]
                        PkT = [kb.sb(es, "PkT%d" % i, [128, 8, 128], F32, keys=[0, 1, 2, 3]) for i in range(2)]
                        Yk = [kb.sb(es, "Yk%d" % i, [128, 8, 128], F32, keys=[0, 1, 2, 3]) for i in range(2)]
                        tM = Pk[0]
                        identf8 = ident_f[:].unsqueeze(1).to_broadcast([128, 8, 128])
                        QKD = kb.sb(es, "QKD", [128, 8, 128], BF16)
                        kq = kb.sb(es, "kq", [128, 4, 256], BF16)
                        ksg = kq
                        qsg = kq
                        kdst = kb.sb(es, "kdst", [128, 8, 64], BF16)
                        vst = kb.sb(es, "vst", [128, 8, 64], BF16)
                        t1 = kb.sb(es, "t1", [128, 8, 64], F32)
                        t2 = kb.sb(es, "t2", [128, 8, 64], F32)
                        rr = kb.sb(es, "rr", [128, 8, 64], BF16)
                        vnew = kb.sb(es, "vnew", [128, 8, 64], BF16)
                        Sf = kb.sb(es, "Sf", [128, 4, 2, 64], F32)
                        Sb = kb.sb(es, "Sb", [128, 4, 2, 64], BF16)
                        for t in range(8):
                            kb.memset("pool", o_dn[:, t, :], 0.0, o_dn.D(t))
                        PAd, PBd, PCd, PDd = kb.PA.D(), kb.PB.D(), kb.PC.D(), kb.PD.D()
                        PA8 = kb.PA[:, :].rearrange("p (h n) -> p h n", h=8)
                        PB8 = kb.PB[:, :].rearrange("p (h n) -> p h n", h=8)
                        PC8 = kb.PC[:, :].rearrange("p (h n) -> p h n", h=8)
                        PD8 = kb.PD[:, :].rearrange("p (h n) -> p h n", h=8)
                        identb8 = ident_b[:].unsqueeze(1).to_broadcast([128, 8, 128])
                        for sq in range(nseq):
                            tb = sq * T
                            if lat:
                                for dr in range(2):
                                    for par in range(2):
                                        dma("sp", Sf[par * 64:(par + 1) * 64, :, dr, :],
                                            std[dr, :, :, :].rearrange("(pr two) k v -> two k pr v", two=2)[par], [], [Sf])
                            else:
                                kb.memset("dve", Sf[:], 0.0, [Sf])
                            cp("pool", Sb[:], Sf[:], [Sf], [Sb])
                            for s in range(nch):

                                cf, cb = s, nch - 1 - s
                                hf = cf % 2
                                hbk = 1 - hf
                                ch = [None, None]
                                ch[hf], ch[hbk] = cf, cb
                                dirh = [None, None]
                                dirh[hf], dirh[hbk] = 0, 1
                                tok = [tb + ch[0] * 64, tb + ch[1] * 64]
                                til = [tok[0] // 128, tok[1] // 128]
                                hs = [slice(0, 64), slice(64, 128)]
                                SL = [(h % 2) * 4 + h // 2 for h in range(8)]
                                for x in range(2):
                                    d_ = dirh[x]
                                    cp("pool", g_st[hs[x], :].rearrange("p (two pr) -> p two pr", two=2),
                                       gb[hs[x], til[x], 16 + d_ * 8:24 + d_ * 8].rearrange("p (pr two) -> p two pr", two=2), [gb], [g_st])
                                    cp("pool", b_st[hs[x], :].rearrange("p (two pr) -> p two pr", two=2),
                                       gb[hs[x], til[x], d_ * 8:d_ * 8 + 8].rearrange("p (pr two) -> p two pr", two=2), [gb], [b_st])
                                    cp("pool", kq[:, :, x * 64:(x + 1) * 64], dkT[:, :, tok[x]:tok[x] + 64], [dkT], [kq])
                                    cp("pool", kq[:, :, 128 + x * 64:128 + (x + 1) * 64], dqT[:, :, tok[x]:tok[x] + 64], [dqT], [kq])
                                    cp("pool", vst[hs[x], :, :].rearrange("p (two pr) d -> p two pr d", two=2),
                                       v_tok[hs[x], til[x], :].rearrange("p (pr two d) -> p two pr d", two=2, d=64), [v_tok], [vst])
                                cp("pool", ghl[:, 0, :], g_st[:], [g_st], [ghl])
                                tt("dve", ghl[:, 1, :], g_st[:], ghl[:, 0, :], ALU.subtract, [g_st, ghl], [ghl])
                                mm(kb.PD[:, 0:8], tri_b[:, hf, :], ghl[:, 0, :], True, False, [tri_b, ghl], PDd)
                                mm(kb.PD[:, 0:8], tri_b[:, hf, :], ghl[:, 1, :], False, True, [tri_b, ghl], PDd)
                                cp("dve", gc_st[:], kb.PD[:, 0:8], PDd, [gc_st])
                                act(egc[:], gc_st[:], AF.Exp, [gc_st], [egc])
                                ck(41 + 1000 * ps)
                                cp("pool", ghl[:, 0, :], gc_st[:], [gc_st], [ghl])
                                tt("dve", ghl[:, 1, :], gc_st[:], ghl[:, 0, :], ALU.subtract, [gc_st, ghl], [ghl])
                                DGh, DGl = Pk[1], PkT[1]
                                DGh_ap = Pk[1][:].rearrange("p h n -> p (h n)").bitcast(BF16)[:, 0:1024].rearrange("p (h n) -> p h n", h=8)
                                DGl_ap = PkT[1][:].rearrange("p h n -> p (h n)").bitcast(BF16)[:, 0:1024].rearrange("p (h n) -> p h n", h=8)
                                tt("dve", DGh_ap, identb8, ghl[:, 0, :].unsqueeze(2).to_broadcast([128, 8, 128]), ALU.mult, [ident_b, ghl], [DGh])
                                tt("pool", DGl_ap, identb8, ghl[:, 1, :].unsqueeze(2).to_broadcast([128, 8, 128]), ALU.mult, [ident_b, ghl], [DGl])
                                for hh in range(2):
                                    mm(kb.PB[:, hh * 512:(hh + 1) * 512], ones_b[:], DGh_ap[:, hh * 4:(hh + 1) * 4, :].rearrange("p h n -> p (h n)"),
                                       True, False, [ones_b, DGh], PBd)
                                    mm(kb.PB[:, hh * 512:(hh + 1) * 512], ones_b[:], DGl_ap[:, hh * 4:(hh + 1) * 4, :].rearrange("p h n -> p (h n)"),
                                       False, True, [ones_b, DGl], PBd)
                                lastf, lastb = hf * 64 + 63, hbk * 64
                                act(egl[:, 0, :], PB8[:, :, lastf], AF.Exp, PBd, [egl])
                                act(egl[:, 1, :], PB8[:, :, lastb], AF.Exp, PBd, [egl])
                                tt("dve", ekd[hs[hf], :], PB8[hs[hf], :, lastf], gc_st[hs[hf], :], ALU.subtract, PBd + [gc_st], [ekd])
                                tt("dve", ekd[hs[hbk], :], PB8[hs[hbk], :, lastb], gc_st[hs[hbk], :], ALU.subtract, PBd + [gc_st], [ekd])
                                act(ekd[:], ekd[:], AF.Exp, [ekd], [ekd])
                                tt("dve", DF[:], PB8, gc_st[:].unsqueeze(2).to_broadcast([128, 8, 128]), ALU.subtract, PBd + [gc_st], [DF])
                                tt("pool", DG[:], DF[:], negi_s[:, hf, :].unsqueeze(1).to_broadcast([128, 8, 128]), ALU.add, [DF, negi_s], [DG])
                                tt("pool", DF[:], DF[:], negs_s[:, hf, :].unsqueeze(1).to_broadcast([128, 8, 128]), ALU.add, [DF, negs_s], [DF])
                                act(DTi[:], DG[:], AF.Exp, [DG], [DTi])
                                act(DTs[:], DF[:], AF.Exp, [DF], [DTs])
                                ck(42 + 1000 * ps)
                                for x in range(2):
                                    tt("dve", kdst[hs[x], :, :].rearrange("p (two pr) d -> p two pr d", two=2),
                                       k_tok[hs[x], til[x], :].rearrange("p (pr two d) -> p two pr d", two=2, d=64),
                                       ekd[hs[x], :].rearrange("p (two pr) -> p two pr", two=2).unsqueeze(3).to_broadcast([64, 2, 4, 64]), ALU.mult, [k_tok, ekd], [kdst])
                                for h in range(8):
                                    pr, hb, sl = h // 2, (h % 2) * 64, SL[h]
                                    mm(PA8[:, sl, :], kq[hb:hb + 64, pr, 0:128], kq[hb:hb + 64, pr, :], True, True, [kq], PAd)
                                tt("dve", tM[:], PA8[:, :, 0:128], DTs[:], ALU.mult, PAd + [DTs], [tM])
                                tt("pool", Pk[0][:], Pk[0][:], b_st[:].unsqueeze(2).to_broadcast([128, 8, 128]), ALU.mult, [Pk[0], b_st], [Pk[0]])
                                tt("dve", QKD[:], PA8[:, :, 128:256], DTi[:], ALU.mult, PAd + [DTi], [QKD])
                                stt(Yk[0][:], Pk[0][:], -1.0, identf8, ALU.mult, ALU.add, [Pk[0], ident_f], [Yk[0]])
                                ck(43 + 1000 * ps)
                                for sl in range(8):
                                    tr(PB8[:, sl, :], Pk[0][:, sl, :], ident_f[:], [Pk[0], ident_f], PBd)
                                cp("dve", PkT[0][:], PB8, PBd, [PkT[0]])
                                ck(44 + 1000 * ps)
                                PAq = [kb.PA.D(q) for q in range(4)]
                                PBq = [kb.PB.D(q // 2) for q in range(4)]
                                qs = [slice(2 * q, 2 * q + 2) for q in range(4)]
                                for q in range(4):
                                    for sl in (2 * q, 2 * q + 1):
                                        mm(PA8[:, sl, 128:256], PkT[0][:, sl, :], Pk[0][:, sl, :], True, True, PkT[0].D(q) + Pk[0].D(q), PAq[q])
                                        mm(PB8[:, sl, :], Pk[0][:, sl, :], PkT[0][:, sl, :], True, True, PkT[0].D(q) + Pk[0].D(q), PBq[q])
                                for q in range(4):
                                    cp("dve", Pk[1][:, qs[q], :], PA8[:, qs[q], 128:256], PAq[q], Pk[1].D(q))
                                    cp("dve", PkT[1][:, qs[q], :], PB8[:, qs[q], :], PBq[q], PkT[1].D(q))
                                ck(442 + 1000 * ps)
                                cur = 1
                                ycur = 0
                                for lev in range(1, 6):
                                    nxt = 1 - cur
                                    last = (lev == 5)
                                    for q in range(4):
                                        for sl in (2 * q, 2 * q + 1):
                                            mm(PA8[:, sl, 0:128], PkT[cur][:, sl, :], Yk[ycur][:, sl, :], True, True,
                                               PkT[cur].D(q) + Yk[ycur].D(q), PAq[q])
                                            if not last:
                                                mm(PA8[:, sl, 128:256], PkT[cur][:, sl, :], Pk[cur][:, sl, :], True, True,
                                                   PkT[cur].D(q) + Pk[cur].D(q), PAq[q])
                                                mm(PB8[:, sl, :], Pk[cur][:, sl, :], PkT[cur][:, sl, :], True, True,
                                                   PkT[cur].D(q) + Pk[cur].D(q), PBq[q])
                                    for q in range(4):
                                        tt("dve", Yk[1 - ycur][:, qs[q], :], PA8[:, qs[q], 0:128], Yk[ycur][:, qs[q], :], ALU.add,
                                           PAq[q] + Yk[ycur].D(q), Yk[1 - ycur].D(q))
                                        if not last:
                                            cp("dve", Pk[nxt][:, qs[q], :], PA8[:, qs[q], 128:256], PAq[q], Pk[nxt].D(q))
                                            cp("dve", PkT[nxt][:, qs[q], :], PB8[:, qs[q], :], PBq[q], PkT[nxt].D(q))
                                    ycur = 1 - ycur
                                    if not last:
                                        cur = nxt
                                cp("pool", DTs[:], Yk[ycur][:], [Yk[ycur]], [DTs])
                                CT = DTs
                                ck(45 + 1000 * ps)
                                KS = kb.PA[:, 0:1024].rearrange("p (b n) -> p b n", b=2)[:, :, 0:256].rearrange("p b (pr d) -> p b pr d", d=64)
                                QS = kb.PB[:, 0:1024].rearrange("p (b n) -> p b n", b=2)[:, :, 0:256].rearrange("p b (pr d) -> p b pr d", d=64)
                                VN = kb.PA[:, 1024:1536].rearrange("p (h d) -> p h d", d=64)
                                IT = kb.PA[:, 1536:2048].rearrange("p (h d) -> p h d", d=64)
                                egc4 = egc[:].rearrange("p (b pr) -> p b pr", b=2).unsqueeze(3).to_broadcast([128, 2, 4, 64])
                                v4 = lambda bf: bf[:].rearrange("p (b pr) d -> p b pr d", b=2)
                                for h in range(8):
                                    pr, hb, par = h // 2, (h % 2) * 64, h % 2
                                    for x in range(2):
                                        mm(KS[hs[x], par, pr, :], kq[hb:hb + 64, pr, x * 64:(x + 1) * 64], Sb[hb:hb + 64, pr, dirh[x], :],
                                           True, True, [ksg, Sb], PAd)
                                tt("dve", v4(t1), KS, egc4, ALU.mult, PAd + [egc], [t1])
                                tt("pool", rr[:], vst[:], t1[:], ALU.subtract, [vst, t1], [rr])
                                for sl in range(8):
                                    mm(VN[:, sl, :], CT[:, sl, :], rr[:, sl, :], True, True, [CT, rr], PAd)
                                tt("dve", vnew[:], VN, b_st[:].unsqueeze(2).to_broadcast([128, 8, 64]), ALU.mult, PAd + [b_st], [vnew])
                                ck(46 + 1000 * ps)
                                for h in range(8):
                                    pr, hb, par = h // 2, (h % 2) * 64, h % 2
                                    for x in range(2):
                                        mm(QS[hs[x], par, pr, :], kq[hb:hb + 64, pr, 128 + x * 64:128 + (x + 1) * 64], Sb[hb:hb + 64, pr, dirh[x], :],
                                           True, True, [qsg, Sb], PBd)
                                for sl in range(8):
                                    mm(IT[:, sl, :], QKD[:, sl, :], vnew[:, sl, :], True, True, [QKD, vnew], PAd)
                                tt("dve", v4(t1), QS, egc4, ALU.mult, PBd + [egc], [t1])
                                tt("dve", t2[:], IT, t1[:], ALU.add, PAd + [t1], [t2])
                                for x in range(2):
                                    tt("pool", o_dn[hs[x], til[x], :].rearrange("p (pr two d) -> p two pr d", two=2, d=64),
                                       o_dn[hs[x], til[x], :].rearrange("p (pr two d) -> p two pr d", two=2, d=64),
                                       t2[hs[x], :, :].rearrange("p (two pr) d -> p two pr d", two=2), ALU.add, [t2] + o_dn.D(til[x]), o_dn.D(til[x]))
                                ck(47 + 1000 * ps)
                                PCs = [kb.PC[:, 0:256].rearrange("p (pr v) -> p pr v", pr=4), kb.PD[:, 0:256].rearrange("p (pr v) -> p pr v", pr=4)]
                                PCsd = [PCd, PDd]
                                for h in range(8):
                                    pr, hb, sl = h // 2, (h % 2) * 64, SL[h]
                                    for x in range(2):
                                        mm(PCs[x][hb:hb + 64, pr, :], kdst[hs[x], sl, :], vnew[hs[x], sl, :], True, True, [kdst, vnew], PCsd[x])
                                for par in range(2):
                                    ps_ = slice(par * 64, par * 64 + 64)
                                    eg = egl[ps_, :, par * 4:(par + 1) * 4].rearrange("p dr pr -> p pr dr")
                                    tt("dve", Sf[ps_, :, :, :], Sf[ps_, :, :, :], eg.unsqueeze(3).to_broadcast([64, 4, 2, 64]), ALU.mult,
                                       [Sf, egl], [Sf])
                                for x in range(2):
                                    tt("dve", Sf[:, :, dirh[x], :], Sf[:, :, dirh[x], :], PCs[x], ALU.add, PCsd[x] + [Sf], [Sf])
                                cp("pool", Sb[:], Sf[:], [Sf], [Sb])
                                ck(20000 + 100 * ps + s)
                            if not lat:
                                seq_g = sq
                                for dr in range(2):
                                    for par in range(2):
                                        dma("sp", news[seq_g, dr, :, :, :].rearrange("(pr two) k v -> two k pr v", two=2)[par],
                                            Sf[par * 64:(par + 1) * 64, :, dr, :], [Sf], [])
                        kb.dump("odn%d" % ps, o_dn[:], [o_dn])
                        ors = kb.sb(es, "ors", [128, 8], F32)
                        osq_ap = tM[:, 0:4, :].rearrange("p h n -> p (h n)")
                        og_ap = DG[:, 0:4, :].rearrange("p h n -> p (h n)")
                        ogb_ap = QKD[:, 0:4, :].rearrange("p h n -> p (h n)")
                        for t in range(8):
                            tt("pool", osq_ap, o_dn[:, t, :], o_dn[:, t, :], ALU.mult, o_dn.D(t), [tM])
                            kb.reduce_sum(ors[:], osq_ap.rearrange("p (h d) -> p h d", d=64), [tM], [ors])
                            act(ors[:], ors[:], AF.Sqrt, [ors], [ors], scale=1.0 / 64.0, bias=EPS)
                            kb.recip(ors[:], ors[:], [ors], [ors])
                            tt("dve", og_ap.rearrange("p (h d) -> p h d", d=64), o_dn[:, t, :].rearrange("p (h d) -> p h d", d=64),
                               ors[:].unsqueeze(2).to_broadcast([128, 8, 64]), ALU.mult, o_dn.D(t) + [ors], [DG])
                            tt("pool", og_ap.rearrange("p (h d) -> p h d", d=64), og_ap.rearrange("p (h d) -> p h d", d=64),
                               dnw[:].unsqueeze(1).to_broadcast([128, 8, 64]), ALU.mult, [DG, dnw], [DG])
                            tt("dve", ogb_ap, og_ap, zs[:, t, :], ALU.mult, [DG, zs], [QKD])
                            for c in range(4):
                                pb, pbd = kb.bank()
                                pbb = pb.bitcast(BF16)
                                tr(pbb[:, 0:128], ogb_ap[:, c * 128:(c + 1) * 128], ident_b[:], [QKD, ident_b], [pbd])
                                cp("dve", oT[:, 1, c, t * 128:(t + 1) * 128], pbb[:, 0:128], [pbd], oT.D((1, t)))
                        kb.barrier()
                        ck(4 + 10 * ps)
                with contextlib.ExitStack() as es:
                    mT = kb.sb(es, "mT", [128, 8, 1024], BF16)
                    wao = kb.sb(es, "wao", [128, 4, 1024], BF16)
                    wdo = kb.sb(es, "wdo", [128, 4, 1024], BF16)
                    wo = kb.sb(es, "wo", [128, 8, 1024], BF16)
                    wg = [kb.sb(es, "wg%d" % i, [128, 8, 256], BF16) for i in range(2)]
                    sg_ = [kb.sb(es, "sg%d" % i, [128, 512], F32) for i in range(2)]
                    tm_ = [kb.sb(es, "tm%d" % i, [128, 512], F32) for i in range(2)]
                    dma("pool", wao[:], w_ao.rearrange("(kc p) n -> p kc n", p=128), [], [wao])
                    dma("pool", wdo[:], w_do.rearrange("(kc p) n -> p kc n", p=128), [], [wdo])
                    dma("pool", wo[:], w_out.rearrange("(kc p) n -> p kc n", p=128), [], [wo])
                    def issue_wg(m_):
                        if m_ >= 8:
                            return
                        wgb_ = wg[m_ % 2]
                        dma("pool", wgb_[:, :, 0:128], w_in[:, 3616 + m_ * 128:3616 + (m_ + 1) * 128].rearrange("(kc p) n -> p kc n", p=128), [], [wgb_])
                        dma("pool", wgb_[:, :, 128:256], w_in[:, 4640 + m_ * 128:4640 + (m_ + 1) * 128].rearrange("(kc p) n -> p kc n", p=128), [], [wgb_])

                    issue_wg(0)
                    for m in range(8):
                        wgb = wg[m % 2]
                        issue_wg(m + 1)
                        for g in range(2):
                            gs = slice(g * 512, (g + 1) * 512)
                            pg = [kb.bank(), kb.bank()]
                            for w in range(2):
                                for kc in range(8):
                                    mm(pg[w][0], wgb[:, kc, w * 128:(w + 1) * 128], hT[:, kc, gs], kc == 0, kc == 7, [wgb, hT], [pg[w][1]])
                                act(sg_[w][:], pg[w][0], AF.Sigmoid, [pg[w][1]], [sg_[w]])
                            pbr = [kb.bank(), kb.bank()]
                            for w in range(2):
                                wsrc = wao if w == 0 else wdo
                                for kc in range(4):
                                    mm(pbr[w][0], wsrc[:, kc, m * 128:(m + 1) * 128], oT[:, w, kc, gs], kc == 0, kc == 3, [wsrc, oT], [pbr[w][1]])
                                tt("dve", tm_[w][:], pbr[w][0], sg_[w][:], ALU.mult, [pbr[w][1], sg_[w]], [tm_[w]])
                            tt("pool", mT[:, m, gs], tm_[0][:], tm_[1][:], ALU.add, [tm_[0], tm_[1]], [mT])
                    kb.dump("mT%d" % ps, mT[:], [mT])

                    def post_norm(t, pyA, pyB, which):
                        (ya, yad), (yb, ybd) = pyA, pyB
                        s2 = kb.sb
                        act(junk[:, 0:512], ya, AF.Square, [yad], [junk] + ssq.D(t), accum=ssq[:, t:t + 1])
                        act(junk[:, 512:1024], yb, AF.Square, [ybd], [junk] + rst.D(t), accum=rst[:, t:t + 1])
                        tt("dve", ssq[:, t:t + 1], ssq[:, t:t + 1], rst[:, t:t + 1], ALU.add, ssq.D(t) + rst.D(t), ssq.D(t))
                        act(rst[:, t:t + 1], ssq[:, t:t + 1], AF.Sqrt, ssq.D(t), rst.D(t), scale=1.0 / 1024.0, bias=EPS)
                        kb.recip(rst[:, t:t + 1], rst[:, t:t + 1], rst.D(t), rst.D(t))
                        for hh, (yy, yd) in enumerate(((ya, yad), (yb, ybd))):
                            cs_ = slice(hh * 512, (hh + 1) * 512)
                            tmp = tm2[hh]
                            stt(tmp[:], yy, rst[:, t:t + 1], Grow[:, ps, which, cs_], ALU.mult, ALU.mult, [yd] + rst.D(t) + Grow.D((ps, which)), [tmp])
                            tt("pool", xs[:, t, cs_], xs[:, t, cs_], tmp[:], ALU.add, [tmp] + xs.D(t), xs.D(t))

                    tm2 = [kb.sb(es, "tm2%d" % i, [128, 512], F32) for i in range(2)]
                    for t in range(8):
                        py = [kb.bank(), kb.bank()]
                        for hh in range(2):
                            for kc in range(8):
                                mm(py[hh][0], mT[:, kc, t * 128:(t + 1) * 128], wo[:, kc, hh * 512:(hh + 1) * 512], kc == 0, kc == 7, [mT, wo], [py[hh][1]])
                        post_norm(t, py[0], py[1], 0)
                    kb.dump("x1_%d" % ps, xs[:], [xs])
                    kb.barrier()
                    ck(5 + 10 * ps)
                with contextlib.ExitStack() as es:
                    aT = kb.sb(es, "aT", [128, 16, 1024], BF16)
                    fTs = kb.sb(es, "fTs", [128, 8, 1024], F32)
                    w1b = [kb.sb(es, "w1b%d" % i, [128, 8, 512], BF16) for i in range(2)]
                    w2b = [kb.sb(es, "w2b%d" % i, [128, 16, 128], BF16) for i in range(2)]
                    rl = [kb.sb(es, "rl%d" % i, [128, 512], BF16) for i in range(2)]
                    tm2 = [kb.sb(es, "tm2f%d" % i, [128, 512], F32) for i in range(2)]
                    h2T = hT
                    for t in range(8):
                        norm_to_T(t, (A2[:, ps, :], modc[:, 24:32, ps]), h2T)
                    w2v = w_ff2.rearrange("(f p) n -> p f n", p=128)
                    cnt = 0
                    items = []
                    for half in range(2):
                        items += [("w1", half, blk) for blk in range(4)]
                        items += [("w2", half, m) for m in range(8)]
                    kcount = {"w1": 0, "w2": 0}
                    wbufs = {}

                    def issue_load(j):
                        kind, half, i = items[j]
                        kk_ = kcount[kind]
                        kcount[kind] += 1
                        if kind == "w1":
                            wb = w1b[kk_ % 2]
                            c0 = half * 2048 + i * 512
                            dma("pool", wb[:], w_ff1[:, c0:c0 + 512].rearrange("(kc p) n -> p kc n", p=128), [], [wb])
                        else:
                            wb = w2b[kk_ % 2]
                            dma("pool", wb[:], w2v[:, half * 16:(half + 1) * 16, i * 128:(i + 1) * 128], [], [wb])
                        wbufs[j] = wb

                    issue_load(0)
                    for jx, (kind, half, i) in enumerate(items):
                        if jx + 1 < len(items):
                            issue_load(jx + 1)
                        wb = wbufs[jx]
                        if kind == "w1":
                            blk = i
                            for j in range(4):
                                f = blk * 4 + j
                                for g in range(2):
                                    gs = slice(g * 512, (g + 1) * 512)
                                    pb, pbd = kb.bank()
                                    for kc in range(8):
                                        mm(pb, wb[:, kc, j * 128:(j + 1) * 128], h2T[:, kc, gs], kc == 0, kc == 7, [wb, h2T], [pbd])
                                    r_ = rl[cnt % 2]
                                    cnt += 1
                                    act(r_[:], pb, AF.Relu, [pbd], [r_])
                                    tt("dve", aT[:, f, gs], r_[:], r_[:], ALU.mult, [r_], [aT])
                        else:
                            m = i
                            for g in range(2):
                                gs = slice(g * 512, (g + 1) * 512)
                                pb, pbd = kb.bank()
                                for f in range(16):
                                    mm(pb, wb[:, f, :], aT[:, f, gs], f == 0, f == 15, [wb, aT], [pbd])
                                if half == 0:
                                    cp("act", fTs[:, m, gs], pb, [pbd], [fTs])
                                else:
                                    tt("dve", fTs[:, m, gs], fTs[:, m, gs], pb, ALU.add, [pbd, fTs], [fTs])
                    for t in range(8):
                        py = [kb.bank(), kb.bank()]
                        for m in range(8):
                            yy, yd = py[m // 4]
                            tr(yy[:, (m % 4) * 128:(m % 4 + 1) * 128], fTs[:, m, t * 128:(t + 1) * 128], ident_f[:], [fTs, ident_f], [yd])
                        post_norm(t, py[0], py[1], 1)
                        dma("sp", Y[ps][t * 128:(t + 1) * 128, :], xs[:, t, :], xs.D(t), [], chan=kb.outch)
                    kb.barrier()
                    ck(6 + 10 * ps)

    except StopBuild:
        pass
    kb.E["sp"].wait_for(kb.outch, kb.outch.count)
    kb.es.close()
    return kb


_CONST = {}


def _consts():
    if not _CONST:
        cos, sins = _rope_tables()
        tri, negi, negs = _dn_masks()
        masks, _ = _get_na_masks()
        _CONST.update(cos=cos, sin=sins, tri=tri, negi=negi, negs=negs,
                      nam=np.ascontiguousarray(masks.reshape(128, -1)), ident=np.eye(128, dtype=np.float32),
                      perm=_rope_perm())
    return _CONST


def make_in_maps(inp):
    C = _consts()
    f = lambda a: np.ascontiguousarray(np.asarray(a, dtype=np.float32))
    w_in = f(inp["w_in"][0])
    shared = dict(
        w_ada=f(inp["w_ada"][0]), b_ada=f(inp["b_ada"][0]),
        nrm=f(np.stack([inp["norm_pre1"][0], inp["norm_post1"][0], inp["norm_pre2"][0], inp["norm_post2"][0]])),
        w_in=w_in, w_qkp=np.ascontiguousarray(w_in[:, C["perm"]]),
        conv_w=f(inp["conv_w"][0]),
        adt=f(np.stack([np.asarray(inp["a_log"][0]).reshape(16), np.asarray(inp["dt_bias"][0]).reshape(16)])),
        dn_norm=f(inp["dn_norm"][0]),
        tb2=np.ascontiguousarray(_rpb_table(f(inp["na_rpb"][0])).reshape(128, -1)),
        w_ao=f(inp["w_ao"][0]), w_do=f(inp["w_do"][0]), w_out=f(inp["w_out"][0]),
        w_ff1=f(inp["w_ff1"][0]), w_ff2=f(inp["w_ff2"][0]),
        ident=C["ident"], tri=C["tri"], negi=C["negi"], negs=C["negs"], nam=C["nam"], cos=C["cos"], sin=C["sin"],
    )
    xp, xsm, c = f(inp["x_prompt"]), f(inp["x_sample"]), f(inp["c"])
    ck, cv, st = f(inp["cache_na_k"]), f(inp["cache_na_v"]), f(inp["state_delta"])
    c_ctx = f(inp["c_ctx"])
    maps = []
    for i in range(8):
        b = i % 4
        m = dict(shared)
        m.update(xc=np.ascontiguousarray(xp[4 * i:4 * i + 4].reshape(1024, 1024)), xl=xsm[b],
                 cvec=np.ascontiguousarray(np.stack([c_ctx, c[b]])), ck=ck[b, 0], cv=cv[b, 0], st=st[b, 0])
        maps.append(m)
    return maps


_PROG = {}


def kernel(**inputs):
    if "kb" not in _PROG:
        _PROG["kb"] = build_program()
    kb = _PROG["kb"]
    maps = make_in_maps(inputs)
    ncores = int(os.environ.get("KCORES", "8"))
    res = run_bass_kernel_spmd(kb.nc, maps[:ncores], core_ids=list(range(ncores)))
    R = list(res.results)
    while len(R) < 8:
        R.append(R[0])
    y_prompt = np.concatenate([R[i]["yc"].reshape(4, 256, 1024) for i in range(8)], axis=0)
    y_sample = np.stack([R[b]["yl"] for b in range(4)], axis=0)
    nk = np.concatenate([R[i]["newk"] for i in range(8)], axis=0)[:, None]
    nv = np.concatenate([R[i]["newv"] for i in range(8)], axis=0)[:, None]
    ns = np.concatenate([R[i]["news"] for i in range(8)], axis=0)[:, None]
    out = (y_prompt.astype(np.float32), y_sample.astype(np.float32), nk.astype(np.float32), nv.astype(np.float32), ns.astype(np.float32))
    if DEBUG:
        _PROG["dbg"] = R
    return out
```
